# Optimizing a Trainium2 kernel written in Bass

```python
import jax, jax.numpy as jnp
from jax import lax
import numpy as np

D_MODEL = 1024
BATCH = 8
SEQ = 2048
DEPTH = 1

D_MIX = D_MODEL
CONV_GROUPS = 8
CONV_DIM = D_MIX // 2
CONV_KSIZE = 3
SB_HEADS = 8
SB_HEAD_DIM = 64
SB_DIM = SB_HEADS * SB_HEAD_DIM
Q_BLOCK = 128
D_FF = 4 * D_MODEL
N_MOD = 6
EPS = 1e-6
IN_SPLITS = (CONV_DIM, 2 * CONV_DIM, 3 * CONV_DIM,
             3 * CONV_DIM + SB_DIM, 3 * CONV_DIM + 2 * SB_DIM)
D_IN = 3 * CONV_DIM + 3 * SB_DIM

kernel_name = "hybrid_shortconv_stickbreaking_block"


def rms_norm(x, g):
    xf = x.astype(jnp.float32)
    y = xf * lax.rsqrt(jnp.mean(xf * xf, axis=-1, keepdims=True) + EPS)
    return (y * g.astype(jnp.float32)).astype(x.dtype)


def modulate(h, shift, scale):
    return h * (1 + scale[:, None, :]) + shift[:, None, :]


def causal_depthwise_conv(u, w):
    return lax.conv_general_dilated(
        u, w[:, None, :].astype(u.dtype), window_strides=(1,),
        padding=[(CONV_KSIZE - 1, 0)],
        dimension_numbers=("NWC", "WIO", "NWC"),
        feature_group_count=u.shape[-1])


def stick_breaking_attention(q, k, v):
    S = q.shape[1]
    scale = SB_HEAD_DIM ** -0.5
    outs = []
    for i in range(S // Q_BLOCK):
        t0, t1 = i * Q_BLOCK, (i + 1) * Q_BLOCK
        qb = q[:, t0:t1].astype(jnp.float32)
        kb = k[:, :t1].astype(jnp.float32)
        vb = v[:, :t1].astype(jnp.float32)
        z = jnp.einsum("bthd,bshd->bhts", qb, kb) * scale
        t_idx = t0 + jnp.arange(Q_BLOCK)[:, None]
        s_idx = jnp.arange(t1)[None, :]
        causal = s_idx < t_idx
        log_beta = jax.nn.log_sigmoid(z)
        log_1mb = jnp.where(causal, log_beta - z, 0.0)
        rc = lax.cumsum(log_1mb, axis=3, reverse=True)
        log_rem = jnp.concatenate(
            [rc[..., 1:], jnp.zeros_like(rc[..., :1])], axis=-1)
        a = jnp.where(causal, jnp.exp(log_beta + log_rem), 0.0)
        o = jnp.einsum("bhts,bshd->bthd", a, vb)
        outs.append(o.astype(v.dtype))
    return jnp.concatenate(outs, axis=1)


def setup_inputs(seed: int = 0) -> dict:
    key = jax.random.key(seed)
    ks = jax.random.split(key, 20)
    f32 = jnp.float32
    nrm = lambda k, shape, s: jax.random.normal(k, shape, f32) * s
    return {
        "x": nrm(ks[0], (BATCH, SEQ, D_MODEL), 1.0),
        "c": nrm(ks[1], (BATCH, D_MODEL), 1.0),
        "w_ada": nrm(ks[2], (D_MODEL, N_MOD * D_MODEL), D_MODEL ** -0.5),
        "b_ada": nrm(ks[3], (N_MOD * D_MODEL,), 0.01),
        "norm1_g": 1.0 + nrm(ks[4], (D_MODEL,), 0.02),
        "w_in": nrm(ks[5], (D_MODEL, D_IN), D_MODEL ** -0.5),
        "conv_w": nrm(ks[6], (CONV_KSIZE, CONV_DIM), CONV_KSIZE ** -0.5),
        "q_norm_g": 1.0 + nrm(ks[7], (SB_HEAD_DIM,), 0.02),
        "k_norm_g": 1.0 + nrm(ks[8], (SB_HEAD_DIM,), 0.02),
        "conv_out_g": 1.0 + nrm(ks[9], (CONV_DIM,), 0.02),
        "attn_out_g": 1.0 + nrm(ks[10], (SB_DIM,), 0.02),
        "w_out": nrm(ks[11], (D_MIX, D_MODEL), D_MIX ** -0.5),
        "norm2_g": 1.0 + nrm(ks[12], (D_MODEL,), 0.02),
        "w_ff1": nrm(ks[13], (D_MODEL, D_FF), D_MODEL ** -0.5),
        "w_ff2": nrm(ks[14], (D_FF, D_MODEL), D_FF ** -0.5),
    }


def reference(x, c, w_ada, b_ada, norm1_g, w_in, conv_w, q_norm_g, k_norm_g,
              conv_out_g, attn_out_g, w_out, norm2_g, w_ff1, w_ff2):
    B, S, D = x.shape
    mod = jax.nn.silu(c) @ w_ada + b_ada
    shift1, scale1, gate1, shift2, scale2, gate2 = jnp.split(mod, N_MOD, axis=-1)

    for _ in range(DEPTH):
        h = modulate(rms_norm(x, norm1_g), shift1, scale1)
        proj = h @ w_in
        b_gate, c_gate, u, q, k, v = jnp.split(proj, IN_SPLITS, axis=-1)

        y_conv = b_gate * causal_depthwise_conv(c_gate * u, conv_w)

        q = rms_norm(q.reshape(B, S, SB_HEADS, SB_HEAD_DIM), q_norm_g)
        k = rms_norm(k.reshape(B, S, SB_HEADS, SB_HEAD_DIM), k_norm_g)
        v = v.reshape(B, S, SB_HEADS, SB_HEAD_DIM)
        y_attn = stick_breaking_attention(q, k, v).reshape(B, S, SB_DIM)

        mix = jnp.concatenate(
            [rms_norm(y_conv, conv_out_g), rms_norm(y_attn, attn_out_g)], axis=-1)
        x = x + gate1[:, None, :] * (mix @ w_out)

        h2 = modulate(rms_norm(x, norm2_g), shift2, scale2)
        f = jnp.square(jax.nn.relu(h2 @ w_ff1)) @ w_ff2
        x = x + gate2[:, None, :] * f
    return x
```

```python
import numpy as np
import ml_dtypes
import concourse.bass as bass
import concourse.mybir as mybir
from concourse.bass_utils import run_bass_kernel_spmd

F32 = mybir.dt.float32
BF16 = mybir.dt.bfloat16
AF = mybir.ActivationFunctionType
ALU = mybir.AluOpType

ENGINES = ("pe", "act", "dve", "pool", "sp")
STRICT_SAME_ENGINE = True


class Buf:
    __slots__ = ("name", "last_writer", "readers", "dead")

    def __init__(self, name):
        self.name = name
        self.last_writer = None
        self.readers = []
        self.dead = False


class Op:
    __slots__ = ("idx", "eng", "fn", "is_dma", "deps", "has_dep", "seq", "dsem", "dval", "name")

    def __init__(self, idx, eng, fn, is_dma, name):
        self.idx = idx
        self.eng = eng
        self.fn = fn
        self.is_dma = is_dma
        self.deps = []
        self.has_dep = False
        self.seq = None
        self.dsem = None
        self.dval = None
        self.name = name


class Prog:
    NDMA_SEMS = 8

    def __init__(self):
        self.ops = []

    def buf(self, name):
        return Buf(name)

    def add(self, eng, fn, reads=(), writes=(), dma=False, name=""):
        op = Op(len(self.ops), eng, fn, dma, name)
        deps = {}
        for b in reads:
            w = b.last_writer
            if w is not None:
                deps[w.idx] = (w, True)
        for b in writes:
            w = b.last_writer
            if w is not None and w.idx not in deps:
                deps[w.idx] = (w, False)
            for r in b.readers:
                if r.idx not in deps:
                    deps[r.idx] = (r, False)
        for b in reads:
            b.readers.append(op)
        for b in writes:
            b.last_writer = op
            b.readers = []
        for d, raw in deps.values():
            if d is op:
                continue
            if (not d.is_dma) and d.eng == eng and not raw and not (STRICT_SAME_ENGINE and eng != "pe"):
                continue
            op.deps.append(d)
            d.has_dep = True
        self.ops.append(op)
        return op

    def alias(self, new_bufs, old_bufs):
        olds = []
        for b in old_bufs:
            if b.last_writer is not None:
                olds.append(b.last_writer)
            olds.extend(b.readers)
        for nb in new_bufs:
            nb.readers = list(olds) + nb.readers

    def emit(self, nc, block, engmap):
        per_eng = {e: [] for e in ENGINES}
        for op in self.ops:
            per_eng[op.eng].append(op)
        counters = {e: 0 for e in ENGINES}
        dma_count = {e: 0 for e in ENGINES}
        for op in self.ops:
            if op.is_dma:
                k = dma_count[op.eng]
                dma_count[op.eng] += 1
                op.dsem = (op.eng, k % self.NDMA_SEMS)
                op.dval = 16 * (k // self.NDMA_SEMS + 1)
            elif op.has_dep:
                counters[op.eng] += 1
                op.seq = counters[op.eng]
        self.stats = {e: len(per_eng[e]) for e in ENGINES}

        csem = self.csem
        dsems = self.dsems

        def run_engine(ename):
            def body(eng):
                waited = {}
                last_dma_on_sem = {}

                def wait(key, sem, val):
                    if waited.get(key, 0) >= val:
                        return
                    waited[key] = val
                    eng.wait_ge(sem, val)

                for op in per_eng[ename]:
                    for d in op.deps:
                        if d.is_dma:
                            wait(("d",) + d.dsem, dsems[d.dsem[0]][d.dsem[1]], d.dval)
                        else:
                            wait(("c", d.eng), csem[d.eng], d.seq)
                    if op.is_dma:
                        if op.dval > 16:
                            wait(("d",) + op.dsem, dsems[op.dsem[0]][op.dsem[1]], op.dval - 16)
                        ins = op.fn(eng)
                        ins.then_inc(dsems[op.dsem[0]][op.dsem[1]], 16)
                        last_dma_on_sem[op.dsem] = op.dval
                    else:
                        ins = op.fn(eng)
                        if op.seq is not None:
                            ins.then_inc(csem[ename], 1)
                for key, val in last_dma_on_sem.items():
                    wait(("d",) + key, dsems[key[0]][key[1]], val)
            return body

        for ename in ENGINES:
            if not per_eng[ename]:
                continue
            getattr(block, engmap[ename])(run_engine(ename))


BLOCK_ATTR = {"pe": "tensor", "act": "scalar", "dve": "vector", "pool": "gpsimd", "sp": "sync"}


def run_prog(nc, prog):
    from contextlib import ExitStack
    with ExitStack() as es:
        prog.csem = {e: es.enter_context(nc.semaphore("c_" + e)) for e in ENGINES}
        prog.dsems = {e: [es.enter_context(nc.semaphore("d_%s_%d" % (e, i))) for i in range(Prog.NDMA_SEMS)]
                      for e in ("sp", "act", "pool")}
        block = es.enter_context(nc.Block())
        prog.emit(nc, block, BLOCK_ATTR)


S = 2048
D = 1024
DIN = 3072
DFF = 4096
NTB = 16
NTT = 4
EPS = 1e-6
KB = 1024
NV = 96
V_C, V_BADA, V_G1, V_G2, V_CONV, V_GQ, V_GK, V_GMIX = 0, 8, 56, 64, 72, 84, 85, 86
C_ID, C_TRI, C_COMP, C_BLK, C_ONES, C_NEGM = 0, 128, 256, 384, 512, 640


ADD_ENG = "dve"
NJUNK = 4


def build(nc, dbg=False, stage=99):
    from contextlib import ExitStack
    P = Prog()
    di = lambda n, s, d: nc.dram_tensor(n, s, d, kind="ExternalInput").ap()
    x_d = di("x", [S, D], F32)
    vecs_d = di("vecs", [128, NV], F32)
    cbf_d = di("cbf", [128, 768], BF16)
    mask_d = di("mask", [128, 128], F32)
    bada_d = di("b_ada", [6 * D], F32)
    wada_d = di("w_ada", [D, 6 * D], F32)
    win_d = di("w_in", [D, DIN], F32)
    wout_d = di("w_out", [D, D], F32)
    wff1_d = di("w_ff1", [D, DFF], F32)
    wff2_d = di("w_ff2", [DFF, D], F32)
    out_d = nc.dram_tensor("out", [S, D], F32, kind="ExternalOutput").ap()
    dbg_d = {}

    def dbg_out(name, shape, dt):
        dbg_d[name] = nc.dram_tensor("dbg_" + name, shape, dt, kind="ExternalOutput").ap()
        return dbg_d[name]

    es = ExitStack()

    def finish():
        run_prog(nc, P)
        es.close()
        return P, dbg_d

    sb = lambda n, s, d: es.enter_context(nc.sbuf_tensor(n, s, d))
    AR = sb("arena", [128, 196 * KB // 2], BF16)
    cbf = sb("cbf_sb", [128, 768], BF16)
    mask = sb("mask_sb", [128, 128], F32)
    vecs = sb("vecs_sb", [128, NV], F32)
    small = sb("small", [128, 256], F32)
    scbf = sb("scbf", [128, 8], BF16)
    scb = sb("scb", [128, 8, 128], BF16)
    gate2_b = sb("gate2_b", [128, D], F32)
    PS = [es.enter_context(nc.psum_tensor("ps%d" % i, [128, 1024], F32)) for i in range(4)]

    def bank(b):
        return PS[b // 2][:, (b % 2) * 512:(b % 2) * 512 + 512]

    def bank_bf(b):
        return PS[b // 2][:, (b % 2) * 512:(b % 2) * 512 + 256].bitcast(BF16)

    def pair(k):
        return PS[k][:].rearrange("p (a b) -> p a b", b=512)

    bankb = [P.buf("bank%d" % i) for i in range(8)]

    SM_MODT = 0
    SM_A1, SM_S1, SM_A2, SM_S2 = 48, 56, 64, 72
    SM_SC32 = 80
    SM_SS1, SM_RS1 = 88, 104
    SM_SS2, SM_RS2 = 120, 136
    SM_RSM = 152
    SM_TMP = 184
    SM_TMP2 = 216
    b_small = {k: P.buf("sm_" + k) for k in ["modT", "A1S1", "A2S2", "sc", "ss1", "rs1", "ss2", "rs2", "rsm", "tmp", "tmp2"]}

    arena_bufs = []

    def view(off, shape, dt):
        n = 1
        for d_ in shape[1:]:
            n *= d_
        nbytes = n * (4 if dt == F32 else 2)
        a = AR[:, off // 2:(off + nbytes) // 2]
        if dt == F32:
            a = a.bitcast(F32)
        if len(shape) == 3:
            a = a.rearrange("p (a b) -> p a b", b=shape[2])
        return a, (off, off + nbytes)

    def abuf(name, rng):
        b = P.buf(name)
        olds = [ob for (lo, hi, ob) in arena_bufs if lo < rng[1] and rng[0] < hi]
        P.alias([b], olds)
        for ob in olds:
            ob.dead = True
        arena_bufs.append((rng[0], rng[1], b))
        return b

    def abufs(names, rng):
        olds = [ob for (lo, hi, ob) in arena_bufs if lo < rng[1] and rng[0] < hi]
        bs = []
        for nm in names:
            b = P.buf(nm)
            P.alias([b], olds)
            bs.append(b)
        for ob in olds:
            ob.dead = True
        for b in bs:
            arena_bufs.append((rng[0], rng[1], b))
        return bs

    def chk(bufs):
        for b in bufs:
            assert not getattr(b, "dead", False), "use of dead buffer " + b.name

    def op(eng, fn, reads=(), writes=(), dma=False, name=""):
        chk(reads)
        chk(writes)
        return P.add(eng, fn, reads=list(reads), writes=list(writes), dma=dma, name=name)

    def dma(eng, out, in_, reads=(), writes=()):
        return op(eng, lambda e: e.dma_start(out=out, in_=in_), reads, writes, dma=True)

    def mm(out, lhsT, rhs, start, stop, reads, writes, **kw):
        return op("pe", lambda e: e.matmul(out, lhsT=lhsT, rhs=rhs, start=start, stop=stop, **kw), reads, writes)

    def act(out, in_, func, reads, writes, **kw):
        return op("act", lambda e: e.activation(out=out, in_=in_, func=func, **kw), reads, writes)

    ident = cbf[:, C_ID:C_ID + 128]
    TRI = cbf[:, C_TRI:C_TRI + 128]
    COMP = cbf[:, C_COMP:C_COMP + 128]
    BLK = cbf[:, C_BLK:C_BLK + 128]
    ONES = cbf[:, C_ONES:C_ONES + 128]
    b_cbf, b_mask, b_vecs, b_scbf, b_scb, b_g2b = [P.buf(n) for n in ["cbf", "mask", "vecs", "scbf", "scb", "g2b"]]

    dma("sp", cbf[:], cbf_d, writes=[b_cbf])
    dma("sp", vecs[:], vecs_d, writes=[b_vecs])
    dma("sp", mask[:], mask_d, writes=[b_mask])
    g1b_v, g1b_r = view(56 * KB, [128, D], F32)
    b_g1b = abuf("gate1_b", g1b_r)
    dma("sp", g1b_v, bada_d[2 * D:3 * D].partition_broadcast(128), writes=[b_g1b])
    dma("sp", gate2_b[:], bada_d[5 * D:6 * D].partition_broadcast(128), writes=[b_g2b])

    sc32 = small[:, SM_SC32:SM_SC32 + 8]
    act(sc32, vecs[:, V_C:V_C + 8], AF.Silu, [b_vecs], [b_small["sc"]])
    op("dve", lambda e: e.tensor_copy(scbf[:], sc32), [b_small["sc"]], [b_scbf])
    for kc in range(8):
        op("dve", lambda e, kc=kc: e.tensor_scalar_mul(out=scb[:, kc, :], in0=ONES, scalar1=small[:, SM_SC32 + kc:SM_SC32 + kc + 1]),
           [b_small["sc"], b_cbf], [b_scb])
    SM_EPS = 251
    b_eps = P.buf("eps")
    op("dve", lambda e: e.memset(small[:, SM_EPS:SM_EPS + 1], EPS), [], [b_eps])
    op("dve", lambda e: e.memset(small[:, SM_SS1:SM_SS1 + 16], 0.0), [], [b_small["ss1"]])
    op("dve", lambda e: e.memset(small[:, SM_SS2:SM_SS2 + 16], 0.0), [], [b_small["ss2"]])

    wa_v, wa_b = [], []
    for off in (144, 152, 96, 104):
        v_, r_ = view(off * KB, [128, 8, 512], BF16)
        wa_v.append(v_)
        wa_b.append(abuf("wa%d" % len(wa_v), r_))
    wa_slot = lambda ci: ci if ci < 4 else ci % 2
    ada_bank = [7]

    def ada_dma(ci):
        wv, wb = wa_v[wa_slot(ci)], wa_b[wa_slot(ci)]
        dma("pool", wv, wada_d[:, ci * 512:(ci + 1) * 512].rearrange("(kc p) n -> p kc n", p=128), writes=[wb])

    def ada_chunk(ci, with_dma=True, part=None):
        wv, wb = wa_v[wa_slot(ci)], wa_b[wa_slot(ci)]
        PB = ada_bank[0]
        if with_dma:
            ada_dma(ci)
        kind = ci // 2
        if kind in (2, 5):
            half = ci % 2
            kcs = range(8) if part is None else range(4 * part, 4 * part + 4)
            for kc in kcs:
                mm(bank(PB), scb[:, kc, :], wv[:, kc, :], kc == 0, kc == 7, [b_scb, wb], [bankb[PB]])
            if part in (None, 1):
                gv, gb = (g1b_v, b_g1b) if kind == 2 else (gate2_b[:], b_g2b)
                op("dve", lambda e: e.tensor_tensor(out=gv[:, half * 512:(half + 1) * 512], in0=bank(PB), in1=gv[:, half * 512:(half + 1) * 512], op=ALU.add),
                   [bankb[PB], gb], [gb])
        else:
            jjs = range(4) if part is None else range(2 * part, 2 * part + 2)
            for jj in jjs:
                for kc in range(8):
                    mm(bank(PB)[:, jj:jj + 1], wv[:, kc, jj * 128:(jj + 1) * 128], scbf[:, kc:kc + 1], kc == 0, kc == 7,
                       [b_scbf, wb], [bankb[PB]], skip_group_check=True)
            if part in (None, 1):
                j0 = 4 * ci
                op("dve", lambda e: e.tensor_tensor(out=small[:, SM_MODT + j0:SM_MODT + j0 + 4], in0=bank(PB)[:, 0:4],
                                                    in1=vecs[:, V_BADA + j0:V_BADA + j0 + 4], op=ALU.add),
                   [bankb[PB], b_vecs], [b_small["modT"]])

    def ada_finish(which):
        base = 0 if which == 1 else 24
        A, Sx, G = (SM_A1, SM_S1, V_G1) if which == 1 else (SM_A2, SM_S2, V_G2)
        key = "A1S1" if which == 1 else "A2S2"
        op("dve", lambda e: e.scalar_tensor_tensor(out=small[:, A:A + 8], in0=small[:, SM_MODT + base + 8:SM_MODT + base + 16], scalar=1.0,
                                                   in1=vecs[:, G:G + 8], op0=ALU.add, op1=ALU.mult),
           [b_small["modT"], b_vecs], [b_small[key]])
        op("dve", lambda e: e.tensor_copy(small[:, Sx:Sx + 8], small[:, SM_MODT + base:SM_MODT + base + 8]),
           [b_small["modT"]], [b_small[key]])

    for ci in range(4):
        ada_dma(ci)

    def make_norm(which, hT_v, hT_b, get_block, xn_views, xn_bufs, junk_of):
        SS, RS = (SM_SS1, SM_RS1) if which == 1 else (SM_SS2, SM_RS2)
        kss, krs = ("ss1", "rs1") if which == 1 else ("ss2", "rs2")
        A, Sx = (SM_A1, SM_S1) if which == 1 else (SM_A2, SM_S2)
        kAS = "A1S1" if which == 1 else "A2S2"
        nb = len(xn_views)

        def stats_act(tb, xv, xb_, jv, jb):
            act(jv, xv, AF.Square, [xb_], [jb, b_small[kss]], accum_out=small[:, SS + tb:SS + tb + 1], scale=float(D) ** -0.5)
            act(small[:, SM_TMP2 + tb:SM_TMP2 + tb + 1], small[:, SS + tb:SS + tb + 1], AF.Ln, [b_small[kss], b_eps], [b_small["tmp2"]],
                bias=small[:, SM_EPS:SM_EPS + 1])
            act(small[:, RS + tb:RS + tb + 1], small[:, SM_TMP2 + tb:SM_TMP2 + tb + 1], AF.Exp, [b_small["tmp2"]], [b_small[krs]], scale=-0.5)

        def stats(tg, skip_act=False):
            xnv, xnb = xn_views[tg % nb], xn_bufs[tg % nb]
            for tbi in range(4):
                tb = 4 * tg + tbi
                xv, xb_ = get_block(tb)
                if not skip_act:
                    jv, jb = junk_of(tg, tbi)
                    stats_act(tb, xv, xb_, jv, jb)
                op("dve", lambda e, tbi=tbi, tb=tb, xv=xv, xnv=xnv: e.tensor_scalar_mul(out=xnv[:, tbi, :], in0=xv, scalar1=small[:, RS + tb:RS + tb + 1]),
                   [xb_, b_small[krs]], [xnb])

        def transp(tg, banks=(0, 1)):
            xnv, xnb = xn_views[tg % nb], xn_bufs[tg % nb]
            for kc in range(8):
                bk = banks[kc % len(banks)]
                for tbi in range(4):
                    op("pe", lambda e, bk=bk, tbi=tbi, kc=kc, xnv=xnv: e.transpose(bank_bf(bk)[:, tbi * 128:(tbi + 1) * 128],
                                                                                 xnv[:, tbi, kc * 128:(kc + 1) * 128], ident),
                       [xnb, b_cbf], [bankb[bk]])
                dst = hT_v[:, kc, tg * 512:(tg + 1) * 512]
                if kc % 2 == 0:
                    act(dst, bank_bf(bk), AF.Identity, [bankb[bk], b_small[kAS]], [hT_b[tg]],
                        scale=small[:, A + kc:A + kc + 1], bias=small[:, Sx + kc:Sx + kc + 1])
                else:
                    op("dve", lambda e, dst=dst, bk=bk, kc=kc: e.tensor_scalar(out=dst, in0=bank_bf(bk), scalar1=small[:, A + kc:A + kc + 1],
                                                                             scalar2=small[:, Sx + kc:Sx + kc + 1], op0=ALU.mult, op1=ALU.add),
                       [bankb[bk], b_small[kAS]], [hT_b[tg]])
        transp.stats_act = stats_act
        return stats, transp

    hT_v, hT_r = view(0, [128, 8, S], BF16)
    hT_b = abufs(["hT%d" % tg for tg in range(NTT)], hT_r)
    xs_v, xs_b = [], []
    for i in range(4):
        v_, r_ = view((176 + 4 * i) * KB, [128, D], F32)
        xs_v.append(v_)
        xs_b.append(abuf("xs%d" % i, r_))
    junk_v, junk_r = view(60 * KB, [128, D], BF16)
    junk_b = abuf("junk", junk_r)
    xn_v, xn_b = [], []
    for i, off in enumerate((160, 168, 122)):
        v_, r_ = view(off * KB, [128, 4, D], BF16)
        xn_v.append(v_)
        xn_b.append(abuf("xn%d" % i, r_))

    def load_x_block(tb):
        i = tb % 4
        dma("sp", xs_v[i], x_d[tb * 128:(tb + 1) * 128, :], writes=[xs_b[i]])
        return xs_v[i], xs_b[i]

    n1_stats, n1_transp = make_norm(1, hT_v, hT_b, load_x_block, xn_v, xn_b, lambda tg, tbi: (junk_v, junk_b))
    wch_v, wch_b = [], []
    for i in range(3):
        v_, r_ = view((32 + 8 * i) * KB, [128, 8, 512], BF16)
        wch_v.append(v_)
        wch_b.append(abuf("wch%d" % i, r_))

    def load_win(fam, slot, after=()):
        dma("pool", wch_v[slot], win_d[:, fam * 512:(fam + 1) * 512].rearrange("(kc p) n -> p kc n", p=128), reads=list(after), writes=[wch_b[slot]])

    SL_OFF = {0: 132 * KB, 1: 136 * KB, 2: 140 * KB, 3: 192 * KB}
    wsl = {}

    def load_qk_slices(c, after=()):
        out = []
        for f_, foff in enumerate((1536, 2048)):
            v_, r_ = view(SL_OFF[c] + 2 * KB * f_, [128, 8, 128], BF16)
            b_ = abuf("wsl%d_%d" % (c, f_), r_)
            dma("pool", v_, win_d[:, foff + c * 128:foff + (c + 1) * 128].rearrange("(kc p) n -> p kc n", p=128), reads=list(after), writes=[b_])
            out.append((v_, b_))
        wsl[c] = out

    load_qk_slices(0)
    load_win(5, 2)

    q_v, q_r = view(64 * KB, [128, 4, S], BF16)
    k_v, k_r = view(80 * KB, [128, 4, S], BF16)
    v_v, v_r = view(96 * KB, [128, NTB, 512], BF16)
    q_b = [abuf("q%d" % c, (64 * KB + c * 4 * KB, 64 * KB + (c + 1) * 4 * KB)) for c in range(4)]
    k_b = [abuf("k%d" % c, (80 * KB + c * 4 * KB, 80 * KB + (c + 1) * 4 * KB)) for c in range(4)]
    PSTAT = 7
    stat_started = [False]

    def stat_mm(col, lhsT, reads):
        st = not stat_started[0]
        stat_started[0] = True
        mm(bank(PSTAT)[:, col:col + 1], lhsT, ONES[:, 0:1], st, False, list(reads) + [b_cbf], [bankb[PSTAT]], skip_group_check=True)

    def proj_fm(bk, wv, wb, col0, tt, kcs=range(8)):
        for kc in kcs:
            mm(bank(bk), wv[:, kc, col0:col0 + 128], hT_v[:, kc, tt * 512:(tt + 1) * 512], kc == 0, kc == 7,
               [wb, hT_b[tt]], [bankb[bk]])

    sq_v, sq_b, lnt_v, lnt_b, rin_v, rin_b = [], [], [], [], [], []
    for i in range(2):
        v_, r_ = view((112 + i) * KB, [128, 512], BF16)
        sq_v.append(v_); sq_b.append(abuf("sq%d" % i, r_))
        v_, r_ = view((114 + 2 * i) * KB, [128, 512], F32)
        lnt_v.append(v_); lnt_b.append(abuf("lnt%d" % i, r_))
        v_, r_ = view((118 + 2 * i) * KB, [128, 512], F32)
        rin_v.append(v_); rin_b.append(abuf("rin%d" % i, r_))
    SM_C64 = 250
    b_c64 = P.buf("c64")
    op("dve", lambda e: e.memset(small[:, SM_C64:SM_C64 + 1], 64.0 * EPS), [], [b_c64])
    QK_FAM = [(q_v, q_b, V_GQ), (k_v, k_b, V_GK)]

    def qk_unit(n_qk, f_, c, tt):
        dst_v, dst_b, gcol = QK_FAM[f_]
        wv, wb = wsl[c][f_]
        bq = 3 + n_qk % 2
        bs = 5 + n_qk % 2
        ti = n_qk % 2
        proj_fm(bq, wv, wb, 0, tt)
        act(sq_v[ti], bank(bq), AF.Square, [bankb[bq]], [sq_b[ti]])
        mm(bank(bs), BLK, sq_v[ti], True, True, [b_cbf, sq_b[ti]], [bankb[bs]])
        act(lnt_v[ti], bank(bs), AF.Ln, [bankb[bs], b_c64], [lnt_b[ti]], bias=small[:, SM_C64:SM_C64 + 1])
        act(rin_v[ti], lnt_v[ti], AF.Exp, [lnt_b[ti]], [rin_b[ti]], scale=-0.5)
        dst = dst_v[:, c, tt * 512:(tt + 1) * 512]
        op("dve", lambda e: e.scalar_tensor_tensor(out=dst, in0=bank(bq), scalar=vecs[:, gcol:gcol + 1],
                                                   in1=rin_v[ti], op0=ALU.mult, op1=ALU.mult),
           [bankb[bq], b_vecs, rin_b[ti]], [dst_b[c]])

    qk_units = []
    n_qk = 0
    for f_ in range(2):
        for tt in range(NTT):
            qk_units.append(lambda n_qk=n_qk, f_=f_, tt=tt: qk_unit(n_qk, f_, 0, tt))
            n_qk += 1
    PB1 = (0, 1, 2, 7)
    n1_stats(0)
    n1_stats(1)
    n1_stats(2)
    for ci in range(4):
        ada_chunk(ci, with_dma=False)
    ada_finish(1)
    n1_transp(0, banks=PB1)
    n1_stats(3)
    n1_transp(1, banks=PB1)
    qk_units.pop(0)()
    n1_transp(2, banks=PB1)
    qk_units.pop(0)()
    n1_transp(3, banks=PB1)
    load_win(1, 1, after=[hT_b[3]])
    load_win(0, 0, after=[hT_b[3]])
    for c in (1, 2, 3):
        load_qk_slices(c, after=[hT_b[3]])
    while qk_units:
        qk_units.pop(0)()
    v_b = abuf("v", v_r)
    for tb in range(NTB):
        bk = tb % 3
        tt = tb // 4
        for kc in range(8):
            mm(bank(bk), hT_v[:, kc, tb * 128:(tb + 1) * 128], wch_v[2][:, kc, :], kc == 0, kc == 7, [hT_b[tt], wch_b[2]], [bankb[bk]])
        if tb % 2 == 0:
            act(v_v[:, tb, :], bank(bk), AF.Copy, [bankb[bk]], [v_b])
        else:
            op("dve", lambda e, tb=tb, bk=bk: e.tensor_copy(v_v[:, tb, :], bank(bk)), [bankb[bk]], [v_b])
    load_win(2, 2)
    if dbg:
        dbg_out("q", [128, 4, S], BF16); dbg_out("k", [128, 4, S], BF16)
        for c in range(4):
            dma("sp", dbg_d["q"][:, c, :], q_v[:, c, :], reads=q_b)
            dma("sp", dbg_d["k"][:, c, :], k_v[:, c, :], reads=k_b)
        dbg_out("v", [128, NTB, 512], BF16)
        for c in range(4):
            dma("sp", dbg_d["v"][:, 4 * c:4 * c + 4, :], v_v[:, 4 * c:4 * c + 4, :], reads=[v_b])
    if stage <= 2:
        return finish()

    SB5, SB6 = 5, 5
    ada_bank[0] = SB5
    yc_v, yc_r = view(112 * KB, [128, 4, S], BF16)
    yc_b = abuf("yc", yc_r)
    csb_v, csb_r = view(184 * KB, [128, 512], F32)
    cu_v, cu_r = view(186 * KB, [128, 2 + 512], F32)
    acc_v, acc_r = view(186 * KB + 2304, [128, 512], F32)
    ysq_v, ysq_r = view(186 * KB + 2304 + 2048, [128, 512], BF16)
    csb_b, cu_b, acc_b, ysq_b = abuf("csb", csb_r), abuf("cu", cu_r), abuf("acc", acc_r), abuf("ysq", ysq_r)
    cw = lambda j, k_: vecs[:, V_CONV + 3 * j + k_:V_CONV + 3 * j + k_ + 1]

    def conv_a1_pe(j, tt, kcs):
        proj_fm(SB5, wch_v[1], wch_b[1], j * 128, tt, kcs)

    def conv_a1_dve(j, tt):
        op("dve", lambda e: e.tensor_copy(csb_v, bank(SB5)), [bankb[SB5]], [csb_b])

    def conv_a2_pe(j, tt, kcs):
        proj_fm(SB6, wch_v[2], wch_b[2], j * 128, tt, kcs)

    def conv_a2_dve(j, tt):
        if tt == 0:
            op("dve", lambda e: e.memset(cu_v[:, 0:2], 0.0), [], [cu_b])
        else:
            op("dve", lambda e: e.tensor_copy(cu_v[:, 0:2], cu_v[:, 512:514]), [cu_b], [cu_b])
        op("dve", lambda e: e.tensor_tensor(out=cu_v[:, 2:514], in0=bank(SB6), in1=csb_v, op=ALU.mult), [bankb[SB6], csb_b], [cu_b])

    def conv_b_pe(j, tt, kcs):
        proj_fm(SB5, wch_v[0], wch_b[0], j * 128, tt, kcs)

    def conv_b_dve1(j, tt):
        op("dve", lambda e: e.tensor_scalar_mul(out=acc_v, in0=cu_v[:, 2:514], scalar1=cw(j, 2)), [cu_b, b_vecs], [acc_b])
        op("dve", lambda e: e.scalar_tensor_tensor(out=acc_v, in0=cu_v[:, 1:513], scalar=cw(j, 1), in1=acc_v, op0=ALU.mult, op1=ALU.add),
           [cu_b, b_vecs, acc_b], [acc_b])
        op("dve", lambda e: e.scalar_tensor_tensor(out=acc_v, in0=cu_v[:, 0:512], scalar=cw(j, 0), in1=acc_v, op0=ALU.mult, op1=ALU.add),
           [cu_b, b_vecs, acc_b], [acc_b])

    def conv_b_dve2(j, tt):
        ydst = yc_v[:, j, tt * 512:(tt + 1) * 512]
        op("dve", lambda e: e.tensor_tensor(out=ydst, in0=bank(SB5), in1=acc_v, op=ALU.mult), [bankb[SB5], acc_b], [yc_b])
        op("dve", lambda e: e.tensor_tensor(out=ysq_v, in0=ydst, in1=ydst, op=ALU.mult), [yc_b], [ysq_b])

    def conv_b_stat(j, tt):
        for tbi in range(4):
            stat_mm(4 * tt + tbi, ysq_v[:, tbi * 128:(tbi + 1) * 128], [ysq_b])

    wo32_holder = {}

    gate_tasks = []

    def wo32_task():
        wo32_v, wo32_r = view(0, [128, 8, D], F32)
        wo32_b = [abuf("wo32_%d" % kc, (kc * 4 * KB, (kc + 1) * 4 * KB)) for kc in range(8)]
        for kc in range(8):
            dma("sp", wo32_v[:, kc, :], wout_d[kc * 128:(kc + 1) * 128, :], writes=[wo32_b[kc]])
        wo32_holder["v"], wo32_holder["b"] = wo32_v, wo32_b
        wo32_holder["countdown"] = 3
        for kc in range(8):
            gate_tasks.append(lambda kc=kc: op("dve", lambda e: e.scalar_tensor_tensor(
                out=wo32_v[:, kc, :], in0=wo32_v[:, kc, :], scalar=vecs[:, V_GMIX + kc:V_GMIX + kc + 1], in1=g1b_v,
                op0=ALU.mult, op1=ALU.mult), [wo32_b[kc], b_vecs, b_g1b], [wo32_b[kc]]))

    side = []
    ada_left = list(range(4, 12))
    nop = lambda: None
    for j in range(4):
        for tt in range(NTT):
            h0, h1 = range(0, 4), range(4, 8)
            side.append((lambda j=j, tt=tt: conv_a1_pe(j, tt, h0), nop))
            side.append((lambda j=j, tt=tt: conv_a1_pe(j, tt, h1), lambda j=j, tt=tt: conv_a1_dve(j, tt)))
            side.append((lambda j=j, tt=tt: conv_a2_pe(j, tt, h0), nop))
            side.append((lambda j=j, tt=tt: conv_a2_pe(j, tt, h1), lambda j=j, tt=tt: conv_a2_dve(j, tt)))
            side.append((lambda j=j, tt=tt: conv_b_pe(j, tt, h0), lambda j=j, tt=tt: conv_b_dve1(j, tt)))
            side.append((lambda j=j, tt=tt: conv_b_pe(j, tt, h1), lambda j=j, tt=tt: conv_b_dve2(j, tt)))
            if tt % 2 == 1 and ada_left:
                ci = ada_left.pop(0)
                side.append((lambda j=j, tt=tt, ci=ci: (conv_b_stat(j, tt), ada_chunk(ci, with_dma=False, part=0)), nop))
                side.append((lambda ci=ci: ada_chunk(ci, with_dma=False, part=1), nop))
            else:
                if tt % 2 == 0 and ada_left:
                    side.append((lambda j=j, tt=tt, ci=ada_left[0]: (conv_b_stat(j, tt), ada_dma(ci)), nop))
                else:
                    side.append((lambda j=j, tt=tt: conv_b_stat(j, tt), nop))
    side.append((lambda: ada_finish(2), nop))

    QB = 6
    sqh_v, sqh_r = view(61 * KB, [128, 256], BF16)
    pqs_v, pqs_r = view(61 * KB + 512, [128, 256], F32)
    lnh_v, lnh_r = view(62 * KB + 512, [128, 256], F32)
    sqh_b, pqs_b, lnh_b = abuf("sqh", sqh_r), abuf("pqs", pqs_r), abuf("lnh", lnh_r)
    pqh = bank(QB)[:, 0:256]
    psh = bank(QB)[:, 256:512]

    def qkh_e1_pe(f_, c, ht):
        wv, wb = wsl[c][f_]
        for kc in range(8):
            mm(pqh, wv[:, kc, :], hT_v[:, kc, ht * 256:(ht + 1) * 256], kc == 0, kc == 7, [wb, hT_b[ht // 2]], [bankb[QB]])

    def qkh_e1_def(f_, c, ht):
        op("dve", lambda e: e.tensor_copy(pqs_v, pqh), [bankb[QB]], [pqs_b])
        op("dve", lambda e: e.tensor_tensor(out=sqh_v, in0=pqs_v, in1=pqs_v, op=ALU.mult), [pqs_b], [sqh_b])

    def qkh_e2_pe(f_, c, ht):
        mm(psh, BLK, sqh_v, True, True, [b_cbf, sqh_b], [bankb[QB]], skip_group_check=True)

    def qkh_e2_def(f_, c, ht):
        dst_v, dst_b, gcol = QK_FAM[f_]
        act(lnh_v, psh, AF.Ln, [bankb[QB], b_c64], [lnh_b], bias=small[:, SM_C64:SM_C64 + 1])
        act(lnh_v, lnh_v, AF.Exp, [lnh_b], [lnh_b], scale=-0.5)
        dst = dst_v[:, c, ht * 256:(ht + 1) * 256]
        op("dve", lambda e: e.scalar_tensor_tensor(out=dst, in0=pqs_v, scalar=vecs[:, gcol:gcol + 1], in1=lnh_v, op0=ALU.mult, op1=ALU.mult),
           [pqs_b, b_vecs, lnh_b], [dst_b[c]])

    side_qk = []
    for c in (1, 2, 3):
        for f_ in range(2):
            for ht in range(8):
                side_qk.append((lambda f_=f_, c=c, ht=ht: qkh_e1_pe(f_, c, ht), lambda f_=f_, c=c, ht=ht: qkh_e1_def(f_, c, ht)))
                side_qk.append((lambda f_=f_, c=c, ht=ht: qkh_e2_pe(f_, c, ht), lambda f_=f_, c=c, ht=ht: qkh_e2_def(f_, c, ht)))

    ya_v, ya_r = view(128 * KB, [128, 4, S], BF16)
    ya_bd = {}

    def ya_buf(c):
        if c not in ya_bd:
            ya_bd[c] = abuf("ya%d" % c, (128 * KB + c * 4 * KB, 128 * KB + (c + 1) * 4 * KB))
        return ya_bd[c]
    u_v, u_b, sp_v, sp_b, a_v, a_b = [], [], [], [], [], []
    for i in range(3):
        v_, r_ = view((160 + 4 * i) * KB, [128, 2, 512], F32)
        u_v.append(v_); u_b.append(abuf("u%d" % i, r_))
    for i in range(2):
        v_, r_ = view((172 + 2 * i) * KB, [128, 2, 512], BF16)
        sp_v.append(v_); sp_b.append(abuf("sp%d" % i, r_))
        v_, r_ = view((176 + 2 * i) * KB, [128, 2, 512], BF16)
        a_v.append(v_); a_b.append(abuf("a%d" % i, r_))
    r_v, r_r = view(180 * KB, [128, 2, 512], F32)
    r_b = abuf("r", r_r)
    yasq_v, yasq_r = view(60 * KB, [128, 512], BF16)
    yasq_b = abuf("yasq", yasq_r)
    ZP = pair(0)
    ZPb = [bankb[0], bankb[1]]
    BP = pair(1)
    BPb = [bankb[2], bankb[3]]
    OBS = [4, 4]
    NEGM = cbf[:, C_NEGM:C_NEGM + 128]

    units = []
    for c in range(4):
        for i in range(NTT):
            kbs = list(range(4 * i + 3, -1, -1))
            for n_, kb in enumerate(kbs):
                units.append(dict(c=c, i=i, kb=kb, first=(n_ == 0), last=(kb == 0), qlo=max(0, kb - 4 * i) * 128, diag=(kb >= 4 * i),
                                  ob=OBS[(c * NTT + i) % 2]))
    NU = len(units)

    def stA(n):
        U = units[n]
        c, i, kb, qlo = U["c"], U["i"], U["kb"], U["qlo"]
        for h in range(2):
            mm(ZP[:, h, qlo:512], k_v[64 * h:64 * h + 64, c, kb * 128:(kb + 1) * 128], q_v[64 * h:64 * h + 64, c, i * 512 + qlo:(i + 1) * 512],
               True, not U["diag"], [k_b[c], q_b[c]], [ZPb[h]], tile_position=(64 * h, 0), skip_group_check=True)
            if U["diag"]:
                mm(ZP[:, h, qlo:qlo + 128], ident, NEGM, False, True, [b_cbf], [ZPb[h]], skip_group_check=True)

    def stB1(n):
        U = units[n]
        qlo = U["qlo"]
        act(u_v[n % 3][:, :, qlo:512], ZP[:, :, qlo:512], AF.Exp, ZPb, [u_b[n % 3]], scale=8.0)

    def stB2(n):
        U = units[n]
        qlo = U["qlo"]
        act(sp_v[n % 2][:, :, qlo:512], u_v[n % 3][:, :, qlo:512], AF.Ln, [u_b[n % 3]], [sp_b[n % 2]], bias=1.0)

    def stC_pe(n):
        U = units[n]
        qlo = U["qlo"]
        for h in range(2):
            mm(BP[:, h, qlo:512], TRI, sp_v[n % 2][:, h, qlo:512], U["first"], False, [b_cbf, sp_b[n % 2]], [BPb[h]], skip_group_check=True)

    def stC_act(n):
        U = units[n]
        qlo = U["qlo"]
        act(r_v[:, :, qlo:512], BP[:, :, qlo:512], AF.Exp, BPb, [r_b], scale=-1.0)

    def stD(n):
        U = units[n]
        qlo = U["qlo"]
        if not U["last"]:
            for h in range(2):
                mm(BP[:, h, qlo:512], COMP, sp_v[n % 2][:, h, qlo:512], False, False, [b_cbf, sp_b[n % 2]], [BPb[h]], skip_group_check=True)
        op("dve", lambda e, n=n, qlo=qlo: e.tensor_tensor(out=a_v[n % 2][:, :, qlo:512], in0=u_v[n % 3][:, :, qlo:512], in1=r_v[:, :, qlo:512], op=ALU.mult),
           [u_b[n % 3], r_b], [a_b[n % 2]])

    def stE(n):
        U = units[n]
        c, i, kb, qlo = U["c"], U["i"], U["kb"], U["qlo"]
        OB = U["ob"]
        for h in range(2):
            hh = 2 * c + h
            mm(bank(OB)[64 * h:64 * h + 64, qlo:512], v_v[:, kb, hh * 64:(hh + 1) * 64], a_v[n % 2][:, h, qlo:512], U["first"], U["last"],
               [v_b, a_b[n % 2]], [bankb[OB]], tile_position=(0, 64 * h), skip_group_check=True)
        if U["last"]:
            dst = ya_v[:, c, i * 512:(i + 1) * 512]
            op("dve", lambda e, dst=dst: e.tensor_copy(dst, bank(OB)), [bankb[OB]], [ya_buf(c)])
            op("dve", lambda e, dst=dst: e.tensor_tensor(out=yasq_v, in0=dst, in1=dst, op=ALU.mult), [ya_buf(c)], [yasq_b])
            pe_later.append([2, lambda i=i: [stat_mm(16 + 4 * i + tbi, yasq_v[:, tbi * 128:(tbi + 1) * 128], [yasq_b]) for tbi in range(4)]])

    pe_later = []

    def junk_mm(k):
        for _ in range(k):
            mm(bank(PSTAT)[:, 64:512], ident, cbf[:, 0:448], False, False, [b_cbf], [bankb[PSTAT]], skip_group_check=True)

    deferred = [[]]
    deferred_late = [[]]
    for step in range(NU + 4):
        for (fn, lag) in ((stD, 3), (stC_pe, 2), (stE, 4)):
            n = step - lag
            if 0 <= n < NU:
                fn(n)
        for ent in list(pe_later):
            if ent[0] == 0:
                ent[1]()
                pe_later.remove(ent)
            else:
                ent[0] -= 1
        for f in deferred.pop(0):
            f()
        if 0 <= step - 1 < NU:
            stB1(step - 1)
        for f in deferred_late.pop(0):
            f()
        for (fn, lag) in ((stC_act, 2), (stB2, 1)):
            n = step - lag
            if 0 <= n < NU:
                fn(n)
        if 0 <= step < NU:
            stA(step)
        late, late2 = [], []
        if step >= 1:
            if side:
                pe_part, late_part = side.pop(0)
                pe_part()
                late.append(late_part)
            if side_qk and step % 4 != 3:
                pe_part, late_part = side_qk.pop(0)
                pe_part()
                late2.append(late_part)
        idle_step = not late and not late2
        if wo32_holder:
            if wo32_holder["countdown"] > 0:
                wo32_holder["countdown"] -= 1
            elif gate_tasks:
                late.append(gate_tasks.pop(0))
        if idle_step:
            if not wo32_holder and not side and not side_qk:
                wo32_task()
            junk_mm(NJUNK)
        deferred.append(late)
        deferred_late.append(late2)
    for f in deferred_late.pop(0):
        f()
    for f in deferred.pop(0):
        f()
    assert not side_qk
    for ent in pe_later:
        ent[1]()
    while side:
        pe_part, late_part = side.pop(0)
        pe_part()
        late_part()
    if not wo32_holder:
        wo32_task()
    while gate_tasks:
        gate_tasks.pop(0)()
    ya_b = [ya_buf(c) for c in range(4)]
    wo32_v, wo32_b = wo32_holder["v"], wo32_holder["b"]
    if dbg:
        dbg_out("yc", [128, 4, S], BF16)
        dbg_out("ya", [128, 4, S], BF16)
        for c in range(4):
            dma("sp", dbg_d["yc"][:, c, :], yc_v[:, c, :], reads=[yc_b])
            dma("sp", dbg_d["ya"][:, c, :], ya_v[:, c, :], reads=ya_b)
    if stage <= 3:
        return finish()

    ffw_off = [144 * KB, 112 * KB]

    def ffn_weight_views(slot):
        w1v, w1r = view(ffw_off[slot], [128, 8, 1024], BF16)
        w2v, w2r = view(ffw_off[slot] + 16 * KB, [128, 8, 1024], BF16)
        return w1v, w1r, w2v, w2r

    def load_ffn_weights(qtr, slot):
        w1v, w1r, w2v, w2r = ffn_weight_views(slot)
        b1, b2 = abuf("w1q%d" % qtr, w1r), abuf("w2q%d" % qtr, w2r)
        for hh in range(2):
            dma("pool", w1v[:, 4 * hh:4 * hh + 4, :],
                wff1_d[512 * hh:512 * hh + 512, qtr * 1024:(qtr + 1) * 1024].rearrange("(kc p) n -> p kc n", p=128), writes=[b1])
        for hh in range(2):
            dma("pool", w2v[:, 4 * hh:4 * hh + 4, :],
                wff2_d[qtr * 1024 + 512 * hh:qtr * 1024 + 512 * hh + 512, :].rearrange("(kc p) n -> p kc n", p=128), writes=[b2])
        return w1v, b1, w2v, b2

    ffw = {0: load_ffn_weights(0, 0)}

    act(small[:, SM_TMP2:SM_TMP2 + 32], bank(PSTAT)[:, 0:32], AF.Ln, [bankb[PSTAT], b_eps], [b_small["tmp2"]],
        scale=1.0 / 512, bias=small[:, SM_EPS:SM_EPS + 1])
    act(small[:, SM_RSM:SM_RSM + 32], small[:, SM_TMP2:SM_TMP2 + 32], AF.Exp, [b_small["tmp2"]], [b_small["rsm"]], scale=-0.5)
    wog_v, wog_r = view(96 * KB, [128, 8, D], BF16)
    wog_b = abufs(["wog%d" % kc for kc in range(8)], wog_r)
    for kc in range(8):
        if kc % 2 == 0:
            act(wog_v[:, kc, :], wo32_v[:, kc, :], AF.Copy, [wo32_b[kc]], [wog_b[kc]])
        else:
            op("dve", lambda e, kc=kc: e.tensor_copy(wog_v[:, kc, :], wo32_v[:, kc, :]), [wo32_b[kc]], [wog_b[kc]])
    xs2_v, xs2_b = [], []
    for i in range(2):
        v_, r_ = view((176 + 4 * i) * KB, [128, D], F32)
        xs2_v.append(v_)
        xs2_b.append(abuf("xs2_%d" % i, r_))
    x1_v, x1_r = view(0, [128, NTB, D], F32)
    x1_b = [abuf("x1_%d" % tb, (tb * 4 * KB, (tb + 1) * 4 * KB)) for tb in range(NTB)]
    junk2_v, junk2_r = view(184 * KB, [128, D], BF16)
    junk2_b = abuf("junk2", junk2_r)
    _n2s, _n2t = make_norm(2, None, None, None, [None], [None], None)
    for tb in range(NTB):
        i = tb % 2
        dma("sp", xs2_v[i], x_d[tb * 128:(tb + 1) * 128, :], writes=[xs2_b[i]])
        for fh in range(2):
            bc, ba = (0, 1) if fh == 0 else (2, 3)
            for j in range(4):
                mm(bank(bc), yc_v[:, j, tb * 128:(tb + 1) * 128], wog_v[:, j, fh * 512:(fh + 1) * 512], j == 0, j == 3, [yc_b, wog_b[j]], [bankb[bc]])
            for c in range(4):
                mm(bank(ba), ya_v[:, c, tb * 128:(tb + 1) * 128], wog_v[:, 4 + c, fh * 512:(fh + 1) * 512], c == 0, c == 3, ya_b + [wog_b[4 + c]], [bankb[ba]])
            dst = x1_v[:, tb, fh * 512:(fh + 1) * 512]
            op("dve", lambda e, dst=dst, bc=bc, tb=tb, i=i, fh=fh: e.scalar_tensor_tensor(
                out=dst, in0=bank(bc), scalar=small[:, SM_RSM + tb:SM_RSM + tb + 1], in1=xs2_v[i][:, fh * 512:(fh + 1) * 512],
                op0=ALU.mult, op1=ALU.add), [bankb[bc], b_small["rsm"], xs2_b[i]], [x1_b[tb]])
            op("dve", lambda e, dst=dst, ba=ba, tb=tb: e.scalar_tensor_tensor(
                out=dst, in0=bank(ba), scalar=small[:, SM_RSM + 16 + tb:SM_RSM + 16 + tb + 1], in1=dst,
                op0=ALU.mult, op1=ALU.add), [bankb[ba], b_small["rsm"], x1_b[tb]], [x1_b[tb]])
        _n2t.stats_act(tb, x1_v[:, tb, :], x1_b[tb], junk2_v, junk2_b)
    if dbg:
        dbg_out("x1", [128, NTB, D], F32)
        for tb in range(NTB):
            dma("sp", dbg_d["x1"][:, tb, :], x1_v[:, tb, :], reads=[x1_b[tb]])
    if stage <= 4:
        return finish()

    h2T_v, h2T_r = view(64 * KB, [128, 8, S], BF16)
    h2T_b = abufs(["h2T%d" % tg for tg in range(NTT)], h2T_r)
    xn2_v, xn2_r = view(176 * KB, [128, 4, D], BF16)
    xn2_b = abuf("xn2", xn2_r)
    ffw[1] = load_ffn_weights(1, 1)
    n2_stats, n2_transp = make_norm(2, h2T_v, h2T_b, lambda tb: (x1_v[:, tb, :], x1_b[tb]), [xn2_v], [xn2_b],
                                    lambda tg, tbi: (xn2_v[:, tbi, :], xn2_b))
    gT_v, gT_b = [], []
    for i in range(2):
        v_, r_ = view((96 + 8 * i) * KB, [128, 8, 512], BF16)
        gT_v.append(v_)
        gT_b.append(abuf("gT%d" % i, r_))
    r32_v, r32_b, tmp_v, tmp_b = [], [], [], []
    for i in range(2):
        v_, r_ = view((184 + 2 * i) * KB, [128, 512], F32)
        r32_v.append(v_); r32_b.append(abuf("r32_%d" % i, r_))
        v_, r_ = view((188 + 2 * i) * KB, [128, 512], F32)
        tmp_v.append(v_); tmp_b.append(abuf("tmpf_%d" % i, r_))
    cnt = dict(f1=0, f2=0)

    def ffn1(qtr, tt):
        w1v, b1, _, _ = ffw[qtr]
        g = (qtr * NTT + tt) % 2
        for ffc in range(8):
            bk = cnt["f1"] % 3
            ri = cnt["f1"] % 2
            cnt["f1"] += 1
            for kc in range(8):
                mm(bank(bk), w1v[:, kc, ffc * 128:(ffc + 1) * 128], h2T_v[:, kc, tt * 512:(tt + 1) * 512], kc == 0, kc == 7,
                   [b1, h2T_b[tt]], [bankb[bk]])
            act(r32_v[ri], bank(bk), AF.Relu, [bankb[bk]], [r32_b[ri]])
            op("dve", lambda e, g=g, ffc=ffc, ri=ri: e.tensor_tensor(out=gT_v[g][:, ffc, :], in0=r32_v[ri], in1=r32_v[ri], op=ALU.mult),
               [r32_b[ri]], [gT_b[g]])

    def ffn2(qtr, tt):
        _, _, w2v, b2 = ffw[qtr]
        g = (qtr * NTT + tt) % 2
        for tbi in range(4):
            tb = 4 * tt + tbi
            for fh in range(2):
                bk = 3 + cnt["f2"] % 3
                ti = cnt["f2"] % 2
                cnt["f2"] += 1
                for ffc in range(8):
                    mm(bank(bk), gT_v[g][:, ffc, tbi * 128:(tbi + 1) * 128], w2v[:, ffc, fh * 512:(fh + 1) * 512], ffc == 0, ffc == 7,
                       [gT_b[g], b2], [bankb[bk]])
                op("dve", lambda e, bk=bk, ti=ti, fh=fh: e.tensor_tensor(out=tmp_v[ti], in0=bank(bk), in1=gate2_b[:, fh * 512:(fh + 1) * 512], op=ALU.mult),
                   [bankb[bk], b_g2b], [tmp_b[ti]])
                dst = x1_v[:, tb, fh * 512:(fh + 1) * 512]
                op(ADD_ENG, lambda e, dst=dst, ti=ti: e.tensor_tensor(out=dst, in0=dst, in1=tmp_v[ti], op=ALU.add), [x1_b[tb], tmp_b[ti]], [x1_b[tb]])
            if qtr == 3:
                dma("sp", out_d[tb * 128:(tb + 1) * 128, :], x1_v[:, tb, :], reads=[x1_b[tb]])

    steps = [(qtr, tt) for qtr in range(4) for tt in range(NTT)]
    n2_stats(0, skip_act=True)
    n2_transp(0, banks=(6, 7))
    n2_stats(1, skip_act=True)
    for s_, (qtr, tt) in enumerate(steps):
        ffn1(qtr, tt)
        if qtr == 0 and tt + 1 < NTT:
            n2_transp(tt + 1, banks=(6, 7))
            if tt + 2 < NTT:
                n2_stats(tt + 2, skip_act=True)
        if s_ >= 1:
            pq, pt = steps[s_ - 1]
            ffn2(pq, pt)
            if pt == NTT - 1 and pq + 2 < 4:
                ffw[pq + 2] = load_ffn_weights(pq + 2, pq % 2)
    ffn2(*steps[-1])

    return finish()


def prep_inputs(inputs):
    f = lambda a: np.ascontiguousarray(np.asarray(a, dtype=np.float32))
    x = f(inputs["x"])
    c = f(inputs["c"])
    b_ada = f(inputs["b_ada"])
    conv_w = f(inputs["conv_w"])
    shared = {
        "b_ada": b_ada,
        "w_ada": f(inputs["w_ada"]),
        "w_in": f(inputs["w_in"]),
        "w_out": f(inputs["w_out"]),
        "w_ff1": f(inputs["w_ff1"]),
        "w_ff2": f(inputs["w_ff2"]),
    }
    cb = np.zeros((128, 768), np.float32)
    ar = np.arange(128)
    cb[:, C_ID:C_ID + 128] = np.eye(128)
    cb[:, C_TRI:C_TRI + 128] = (ar[:, None] >= ar[None, :])
    cb[:, C_COMP:C_COMP + 128] = (ar[:, None] < ar[None, :])
    cb[:, C_BLK:C_BLK + 128] = ((ar[:, None] // 64) == (ar[None, :] // 64))
    cb[:, C_ONES:C_ONES + 128] = 1.0
    cb[:, C_NEGM:C_NEGM + 128] = np.where(ar[:, None] >= ar[None, :], -30000.0, 0.0)
    shared["cbf"] = cb.astype(ml_dtypes.bfloat16)
    shared["mask"] = (ar[:, None] < ar[None, :]).astype(np.float32)
    col = lambda v, n: np.asarray(v, np.float32).reshape(n, 128).T
    in_maps = []
    for b in range(x.shape[0]):
        vecs = np.zeros((128, NV), np.float32)
        vecs[:, V_C:V_C + 8] = col(c[b], 8)
        vecs[:, V_BADA:V_BADA + 48] = col(b_ada, 48)
        vecs[:, V_G1:V_G1 + 8] = col(inputs["norm1_g"], 8)
        vecs[:, V_G2:V_G2 + 8] = col(inputs["norm2_g"], 8)
        vecs[:, V_CONV:V_CONV + 12] = conv_w.reshape(3, 4, 128).transpose(2, 1, 0).reshape(128, 12)
        vecs[:, V_GQ] = np.tile(f(inputs["q_norm_g"]), 2)
        vecs[:, V_GK] = np.tile(f(inputs["k_norm_g"]), 2)
        vecs[:, V_GMIX:V_GMIX + 8] = col(np.concatenate([f(inputs["conv_out_g"]), f(inputs["attn_out_g"])]), 8)
        m = dict(shared)
        m["x"] = x[b]
        m["vecs"] = vecs
        in_maps.append(m)
    return in_maps


_CACHE = {}


def kernel(**inputs):
    in_maps = prep_inputs(inputs)
    if "nc" not in _CACHE:
        nc = bass.Bass("TRN2", target_bir_lowering=False)
        build(nc)
        _CACHE["nc"] = nc
    nc = _CACHE["nc"]
    res = run_bass_kernel_spmd(nc, in_maps, core_ids=list(range(8)))
    out = np.stack([np.asarray(r["out"], dtype=np.float32) for r in res.results], axis=0)
    return out
```

```python
import numpy as np
import ml_dtypes
import concourse.bass as bass
import concourse.mybir as mybir
from concourse.bass_utils import run_bass_kernel_spmd

F32 = mybir.dt.float32
BF16 = mybir.dt.bfloat16
AF = mybir.ActivationFunctionType
ALU = mybir.AluOpType

ENGINES = ("pe", "act", "dve", "pool", "sp")
STRICT_SAME_ENGINE = True


class Buf:
    __slots__ = ("name", "last_writer", "readers", "dead")

    def __init__(self, name):
        self.name = name
        self.last_writer = None
        self.readers = []
        self.dead = False


class Op:
    __slots__ = ("idx", "eng", "fn", "is_dma", "deps", "has_dep", "seq", "dsem", "dval", "name")

    def __init__(self, idx, eng, fn, is_dma, name):
        self.idx = idx
        self.eng = eng
        self.fn = fn
        self.is_dma = is_dma
        self.deps = []
        self.has_dep = False
        self.seq = None
        self.dsem = None
        self.dval = None
        self.name = name


class Prog:
    NDMA_SEMS = 8

    def __init__(self):
        self.ops = []

    def buf(self, name):
        return Buf(name)

    def add(self, eng, fn, reads=(), writes=(), dma=False, name=""):
        op = Op(len(self.ops), eng, fn, dma, name)
        deps = {}
        for b in reads:
            w = b.last_writer
            if w is not None:
                deps[w.idx] = (w, True)
        for b in writes:
            w = b.last_writer
            if w is not None and w.idx not in deps:
                deps[w.idx] = (w, False)
            for r in b.readers:
                if r.idx not in deps:
                    deps[r.idx] = (r, False)
        for b in reads:
            b.readers.append(op)
        for b in writes:
            b.last_writer = op
            b.readers = []
        for d, raw in deps.values():
            if d is op:
                continue
            if (not d.is_dma) and d.eng == eng and not raw and not (STRICT_SAME_ENGINE and eng != "pe"):
                continue
            op.deps.append(d)
            d.has_dep = True
        self.ops.append(op)
        return op

    def alias(self, new_bufs, old_bufs):
        olds = []
        for b in old_bufs:
            if b.last_writer is not None:
                olds.append(b.last_writer)
            olds.extend(b.readers)
        for nb in new_bufs:
            nb.readers = list(olds) + nb.readers

    def emit(self, nc, block, engmap):
        per_eng = {e: [] for e in ENGINES}
        for op in self.ops:
            per_eng[op.eng].append(op)
        counters = {e: 0 for e in ENGINES}
        dma_count = {e: 0 for e in ENGINES}
        for op in self.ops:
            if op.is_dma:
                k = dma_count[op.eng]
                dma_count[op.eng] += 1
                op.dsem = (op.eng, k % self.NDMA_SEMS)
                op.dval = 16 * (k // self.NDMA_SEMS + 1)
            elif op.has_dep:
                counters[op.eng] += 1
                op.seq = counters[op.eng]
        self.stats = {e: len(per_eng[e]) for e in ENGINES}

        csem = self.csem
        dsems = self.dsems

        def run_engine(ename):
            def body(eng):
                waited = {}
                last_dma_on_sem = {}

                def wait(key, sem, val):
                    if waited.get(key, 0) >= val:
                        return
                    waited[key] = val
                    eng.wait_ge(sem, val)

                for op in per_eng[ename]:
                    for d in op.deps:
                        if d.is_dma:
                            wait(("d",) + d.dsem, dsems[d.dsem[0]][d.dsem[1]], d.dval)
                        else:
                            wait(("c", d.eng), csem[d.eng], d.seq)
                    if op.is_dma:
                        if op.dval > 16:
                            wait(("d",) + op.dsem, dsems[op.dsem[0]][op.dsem[1]], op.dval - 16)
                        ins = op.fn(eng)
                        ins.then_inc(dsems[op.dsem[0]][op.dsem[1]], 16)
                        last_dma_on_sem[op.dsem] = op.dval
                    else:
                        ins = op.fn(eng)
                        if op.seq is not None:
                            ins.then_inc(csem[ename], 1)
                for key, val in last_dma_on_sem.items():
                    wait(("d",) + key, dsems[key[0]][key[1]], val)
            return body

        for ename in ENGINES:
            if not per_eng[ename]:
                continue
            getattr(block, engmap[ename])(run_engine(ename))


BLOCK_ATTR = {"pe": "tensor", "act": "scalar", "dve": "vector", "pool": "gpsimd", "sp": "sync"}


def run_prog(nc, prog):
    from contextlib import ExitStack
    with ExitStack() as es:
        prog.csem = {e: es.enter_context(nc.semaphore("c_" + e)) for e in ENGINES}
        prog.dsems = {e: [es.enter_context(nc.semaphore("d_%s_%d" % (e, i))) for i in range(Prog.NDMA_SEMS)]
                      for e in ("sp", "act", "pool")}
        block = es.enter_context(nc.Block())
        prog.emit(nc, block, BLOCK_ATTR)


S = 2048
D = 1024
DIN = 3072
DFF = 4096
NTB = 16
NTT = 4
EPS = 1e-6
KB = 1024
NV = 96
V_C, V_BADA, V_G1, V_G2, V_CONV, V_GQ, V_GK, V_GMIX = 0, 8, 56, 64, 72, 84, 85, 86
C_ID, C_TRI, C_COMP, C_BLK, C_ONES, C_NEGM = 0, 128, 256, 384, 512, 640


ADD_ENG = "dve"
NJUNK = 4


def build(nc, dbg=False, stage=99):
    from contextlib import ExitStack
    P = Prog()
    di = lambda n, s, d: nc.dram_tensor(n, s, d, kind="ExternalInput").ap()
    x_d = di("x", [S, D], F32)
    vecs_d = di("vecs", [128, NV], F32)
    cbf_d = di("cbf", [128, 768], BF16)
    mask_d = di("mask", [128, 128], F32)
    bada_d = di("b_ada", [6 * D], F32)
    wada_d = di("w_ada", [D, 6 * D], F32)
    win_d = di("w_in", [D, DIN], F32)
    wout_d = di("w_out", [D, D], F32)
    wff1_d = di("w_ff1", [D, DFF], F32)
    wff2_d = di("w_ff2", [DFF, D], F32)
    out_d = nc.dram_tensor("out", [S, D], F32, kind="ExternalOutput").ap()
    dbg_d = {}

    def dbg_out(name, shape, dt):
        dbg_d[name] = nc.dram_tensor("dbg_" + name, shape, dt, kind="ExternalOutput").ap()
        return dbg_d[name]

    es = ExitStack()

    def finish():
        run_prog(nc, P)
        es.close()
        return P, dbg_d

    sb = lambda n, s, d: es.enter_context(nc.sbuf_tensor(n, s, d))
    AR = sb("arena", [128, 196 * KB // 2], BF16)
    cbf = sb("cbf_sb", [128, 768], BF16)
    mask = sb("mask_sb", [128, 128], F32)
    vecs = sb("vecs_sb", [128, NV], F32)
    small = sb("small", [128, 256], F32)
    scbf = sb("scbf", [128, 8], BF16)
    scb = sb("scb", [128, 8, 128], BF16)
    gate2_b = sb("gate2_b", [128, D], F32)
    PS = [es.enter_context(nc.psum_tensor("ps%d" % i, [128, 1024], F32)) for i in range(4)]

    def bank(b):
        return PS[b // 2][:, (b % 2) * 512:(b % 2) * 512 + 512]

    def bank_bf(b):
        return PS[b // 2][:, (b % 2) * 512:(b % 2) * 512 + 256].bitcast(BF16)

    def pair(k):
        return PS[k][:].rearrange("p (a b) -> p a b", b=512)

    bankb = [P.buf("bank%d" % i) for i in range(8)]

    SM_MODT = 0
    SM_A1, SM_S1, SM_A2, SM_S2 = 48, 56, 64, 72
    SM_SC32 = 80
    SM_SS1, SM_RS1 = 88, 104
    SM_SS2, SM_RS2 = 120, 136
    SM_RSM = 152
    SM_TMP = 184
    SM_TMP2 = 216
    b_small = {k: P.buf("sm_" + k) for k in ["modT", "A1S1", "A2S2", "sc", "ss1", "rs1", "ss2", "rs2", "rsm", "tmp", "tmp2"]}

    arena_bufs = []

    def view(off, shape, dt):
        n = 1
        for d_ in shape[1:]:
            n *= d_
        nbytes = n * (4 if dt == F32 else 2)
        a = AR[:, off // 2:(off + nbytes) // 2]
        if dt == F32:
            a = a.bitcast(F32)
        if len(shape) == 3:
            a = a.rearrange("p (a b) -> p a b", b=shape[2])
        return a, (off, off + nbytes)

    def abuf(name, rng):
        b = P.buf(name)
        olds = [ob for (lo, hi, ob) in arena_bufs if lo < rng[1] and rng[0] < hi]
        P.alias([b], olds)
        for ob in olds:
            ob.dead = True
        arena_bufs.append((rng[0], rng[1], b))
        return b

    def abufs(names, rng):
        olds = [ob for (lo, hi, ob) in arena_bufs if lo < rng[1] and rng[0] < hi]
        bs = []
        for nm in names:
            b = P.buf(nm)
            P.alias([b], olds)
            bs.append(b)
        for ob in olds:
            ob.dead = True
        for b in bs:
            arena_bufs.append((rng[0], rng[1], b))
        return bs

    def chk(bufs):
        for b in bufs:
            assert not getattr(b, "dead", False), "use of dead buffer " + b.name

    def op(eng, fn, reads=(), writes=(), dma=False, name=""):
        chk(reads)
        chk(writes)
        return P.add(eng, fn, reads=list(reads), writes=list(writes), dma=dma, name=name)

    def dma(eng, out, in_, reads=(), writes=()):
        return op(eng, lambda e: e.dma_start(out=out, in_=in_), reads, writes, dma=True)

    def mm(out, lhsT, rhs, start, stop, reads, writes, **kw):
        return op("pe", lambda e: e.matmul(out, lhsT=lhsT, rhs=rhs, start=start, stop=stop, **kw), reads, writes)

    def act(out, in_, func, reads, writes, **kw):
        return op("act", lambda e: e.activation(out=out, in_=in_, func=func, **kw), reads, writes)

    ident = cbf[:, C_ID:C_ID + 128]
    TRI = cbf[:, C_TRI:C_TRI + 128]
    COMP = cbf[:, C_COMP:C_COMP + 128]
    BLK = cbf[:, C_BLK:C_BLK + 128]
    ONES = cbf[:, C_ONES:C_ONES + 128]
    b_cbf, b_mask, b_vecs, b_scbf, b_scb, b_g2b = [P.buf(n) for n in ["cbf", "mask", "vecs", "scbf", "scb", "g2b"]]

    dma("sp", cbf[:], cbf_d, writes=[b_cbf])
    dma("sp", vecs[:], vecs_d, writes=[b_vecs])
    dma("sp", mask[:], mask_d, writes=[b_mask])
    g1b_v, g1b_r = view(56 * KB, [128, D], F32)
    b_g1b = abuf("gate1_b", g1b_r)
    dma("sp", g1b_v, bada_d[2 * D:3 * D].partition_broadcast(128), writes=[b_g1b])
    dma("sp", gate2_b[:], bada_d[5 * D:6 * D].partition_broadcast(128), writes=[b_g2b])

    sc32 = small[:, SM_SC32:SM_SC32 + 8]
    act(sc32, vecs[:, V_C:V_C + 8], AF.Silu, [b_vecs], [b_small["sc"]])
    op("dve", lambda e: e.tensor_copy(scbf[:], sc32), [b_small["sc"]], [b_scbf])
    for kc in range(8):
        op("dve", lambda e, kc=kc: e.tensor_scalar_mul(out=scb[:, kc, :], in0=ONES, scalar1=small[:, SM_SC32 + kc:SM_SC32 + kc + 1]),
           [b_small["sc"], b_cbf], [b_scb])
    SM_EPS = 251
    b_eps = P.buf("eps")
    op("dve", lambda e: e.memset(small[:, SM_EPS:SM_EPS + 1], EPS), [], [b_eps])
    op("dve", lambda e: e.memset(small[:, SM_SS1:SM_SS1 + 16], 0.0), [], [b_small["ss1"]])
    op("dve", lambda e: e.memset(small[:, SM_SS2:SM_SS2 + 16], 0.0), [], [b_small["ss2"]])

    wa_v, wa_b = [], []
    for off in (144, 152, 96, 104):
        v_, r_ = view(off * KB, [128, 8, 512], BF16)
        wa_v.append(v_)
        wa_b.append(abuf("wa%d" % len(wa_v), r_))
    wa_slot = lambda ci: ci if ci < 4 else ci % 2
    ada_bank = [7]

    def ada_dma(ci):
        wv, wb = wa_v[wa_slot(ci)], wa_b[wa_slot(ci)]
        dma("pool", wv, wada_d[:, ci * 512:(ci + 1) * 512].rearrange("(kc p) n -> p kc n", p=128), writes=[wb])

    def ada_chunk(ci, with_dma=True, part=None):
        wv, wb = wa_v[wa_slot(ci)], wa_b[wa_slot(ci)]
        PB = ada_bank[0]
        if with_dma:
            ada_dma(ci)
        kind = ci // 2
        if kind in (2, 5):
            half = ci % 2
            kcs = range(8) if part is None else range(4 * part, 4 * part + 4)
            for kc in kcs:
                mm(bank(PB), scb[:, kc, :], wv[:, kc, :], kc == 0, kc == 7, [b_scb, wb], [bankb[PB]])
            if part in (None, 1):
                gv, gb = (g1b_v, b_g1b) if kind == 2 else (gate2_b[:], b_g2b)
                op("dve", lambda e: e.tensor_tensor(out=gv[:, half * 512:(half + 1) * 512], in0=bank(PB), in1=gv[:, half * 512:(half + 1) * 512], op=ALU.add),
                   [bankb[PB], gb], [gb])
        else:
            jjs = range(4) if part is None else range(2 * part, 2 * part + 2)
            for jj in jjs:
                for kc in range(8):
                    mm(bank(PB)[:, jj:jj + 1], wv[:, kc, jj * 128:(jj + 1) * 128], scbf[:, kc:kc + 1], kc == 0, kc == 7,
                       [b_scbf, wb], [bankb[PB]], skip_group_check=True)
            if part in (None, 1):
                j0 = 4 * ci
                op("dve", lambda e: e.tensor_tensor(out=small[:, SM_MODT + j0:SM_MODT + j0 + 4], in0=bank(PB)[:, 0:4],
                                                    in1=vecs[:, V_BADA + j0:V_BADA + j0 + 4], op=ALU.add),
                   [bankb[PB], b_vecs], [b_small["modT"]])

    def ada_finish(which):
        base = 0 if which == 1 else 24
        A, Sx, G = (SM_A1, SM_S1, V_G1) if which == 1 else (SM_A2, SM_S2, V_G2)
        key = "A1S1" if which == 1 else "A2S2"
        op("dve", lambda e: e.scalar_tensor_tensor(out=small[:, A:A + 8], in0=small[:, SM_MODT + base + 8:SM_MODT + base + 16], scalar=1.0,
                                                   in1=vecs[:, G:G + 8], op0=ALU.add, op1=ALU.mult),
           [b_small["modT"], b_vecs], [b_small[key]])
        op("dve", lambda e: e.tensor_copy(small[:, Sx:Sx + 8], small[:, SM_MODT + base:SM_MODT + base + 8]),
           [b_small["modT"]], [b_small[key]])

    for ci in range(4):
        ada_dma(ci)

    def make_norm(which, hT_v, hT_b, get_block, xn_views, xn_bufs, junk_of):
        SS, RS = (SM_SS1, SM_RS1) if which == 1 else (SM_SS2, SM_RS2)
        kss, krs = ("ss1", "rs1") if which == 1 else ("ss2", "rs2")
        A, Sx = (SM_A1, SM_S1) if which == 1 else (SM_A2, SM_S2)
        kAS = "A1S1" if which == 1 else "A2S2"
        nb = len(xn_views)

        def stats_act(tb, xv, xb_, jv, jb):
            act(jv, xv, AF.Square, [xb_], [jb, b_small[kss]], accum_out=small[:, SS + tb:SS + tb + 1], scale=float(D) ** -0.5)
            act(small[:, SM_TMP2 + tb:SM_TMP2 + tb + 1], small[:, SS + tb:SS + tb + 1], AF.Ln, [b_small[kss], b_eps], [b_small["tmp2"]],
                bias=small[:, SM_EPS:SM_EPS + 1])
            act(small[:, RS + tb:RS + tb + 1], small[:, SM_TMP2 + tb:SM_TMP2 + tb + 1], AF.Exp, [b_small["tmp2"]], [b_small[krs]], scale=-0.5)

        def stats(tg, skip_act=False):
            xnv, xnb = xn_views[tg % nb], xn_bufs[tg % nb]
            for tbi in range(4):
                tb = 4 * tg + tbi
                xv, xb_ = get_block(tb)
                if not skip_act:
                    jv, jb = junk_of(tg, tbi)
                    stats_act(tb, xv, xb_, jv, jb)
                op("dve", lambda e, tbi=tbi, tb=tb, xv=xv, xnv=xnv: e.tensor_scalar_mul(out=xnv[:, tbi, :], in0=xv, scalar1=small[:, RS + tb:RS + tb + 1]),
                   [xb_, b_small[krs]], [xnb])

        def transp(tg, banks=(0, 1)):
            xnv, xnb = xn_views[tg % nb], xn_bufs[tg % nb]
            for kc in range(8):
                bk = banks[kc % len(banks)]
                for tbi in range(4):
                    op("pe", lambda e, bk=bk, tbi=tbi, kc=kc, xnv=xnv: e.transpose(bank_bf(bk)[:, tbi * 128:(tbi + 1) * 128],
                                                                                 xnv[:, tbi, kc * 128:(kc + 1) * 128], ident),
                       [xnb, b_cbf], [bankb[bk]])
                dst = hT_v[:, kc, tg * 512:(tg + 1) * 512]
                if kc % 2 == 0:
                    act(dst, bank_bf(bk), AF.Identity, [bankb[bk], b_small[kAS]], [hT_b[tg]],
                        scale=small[:, A + kc:A + kc + 1], bias=small[:, Sx + kc:Sx + kc + 1])
                else:
                    op("dve", lambda e, dst=dst, bk=bk, kc=kc: e.tensor_scalar(out=dst, in0=bank_bf(bk), scalar1=small[:, A + kc:A + kc + 1],
                                                                             scalar2=small[:, Sx + kc:Sx + kc + 1], op0=ALU.mult, op1=ALU.add),
                       [bankb[bk], b_small[kAS]], [hT_b[tg]])
        transp.stats_act = stats_act
        return stats, transp

    hT_v, hT_r = view(0, [128, 8, S], BF16)
    hT_b = abufs(["hT%d" % tg for tg in range(NTT)], hT_r)
    xs_v, xs_b = [], []
    for i in range(4):
        v_, r_ = view((176 + 4 * i) * KB, [128, D], F32)
        xs_v.append(v_)
        xs_b.append(abuf("xs%d" % i, r_))
    junk_v, junk_r = view(60 * KB, [128, D], BF16)
    junk_b = abuf("junk", junk_r)
    xn_v, xn_b = [], []
    for i, off in enumerate((160, 168, 122)):
        v_, r_ = view(off * KB, [128, 4, D], BF16)
        xn_v.append(v_)
        xn_b.append(abuf("xn%d" % i, r_))

    def load_x_block(tb):
        i = tb % 4
        dma("sp", xs_v[i], x_d[tb * 128:(tb + 1) * 128, :], writes=[xs_b[i]])
        return xs_v[i], xs_b[i]

    n1_stats, n1_transp = make_norm(1, hT_v, hT_b, load_x_block, xn_v, xn_b, lambda tg, tbi: (junk_v, junk_b))
    wch_v, wch_b = [], []
    for i in range(3):
        v_, r_ = view((32 + 8 * i) * KB, [128, 8, 512], BF16)
        wch_v.append(v_)
        wch_b.append(abuf("wch%d" % i, r_))

    def load_win(fam, slot, after=()):
        dma("pool", wch_v[slot], win_d[:, fam * 512:(fam + 1) * 512].rearrange("(kc p) n -> p kc n", p=128), reads=list(after), writes=[wch_b[slot]])

    SL_OFF = {0: 132 * KB, 1: 136 * KB, 2: 140 * KB, 3: 192 * KB}
    wsl = {}

    def load_qk_slices(c, after=()):
        out = []
        for f_, foff in enumerate((1536, 2048)):
            v_, r_ = view(SL_OFF[c] + 2 * KB * f_, [128, 8, 128], BF16)
            b_ = abuf("wsl%d_%d" % (c, f_), r_)
            dma("pool", v_, win_d[:, foff + c * 128:foff + (c + 1) * 128].rearrange("(kc p) n -> p kc n", p=128), reads=list(after), writes=[b_])
            out.append((v_, b_))
        wsl[c] = out

    load_qk_slices(0)
    load_win(5, 2)

    q_v, q_r = view(64 * KB, [128, 4, S], BF16)
    k_v, k_r = view(80 * KB, [128, 4, S], BF16)
    v_v, v_r = view(96 * KB, [128, NTB, 512], BF16)
    q_b = [abuf("q%d" % c, (64 * KB + c * 4 * KB, 64 * KB + (c + 1) * 4 * KB)) for c in range(4)]
    k_b = [abuf("k%d" % c, (80 * KB + c * 4 * KB, 80 * KB + (c + 1) * 4 * KB)) for c in range(4)]
    PSTAT = 7
    stat_started = [False]

    def stat_mm(col, lhsT, reads):
        st = not stat_started[0]
        stat_started[0] = True
        mm(bank(PSTAT)[:, col:col + 1], lhsT, ONES[:, 0:1], st, False, list(reads) + [b_cbf], [bankb[PSTAT]], skip_group_check=True)

    def proj_fm(bk, wv, wb, col0, tt, kcs=range(8)):
        for kc in kcs:
            mm(bank(bk), wv[:, kc, col0:col0 + 128], hT_v[:, kc, tt * 512:(tt + 1) * 512], kc == 0, kc == 7,
               [wb, hT_b[tt]], [bankb[bk]])

    sq_v, sq_b, lnt_v, lnt_b, rin_v, rin_b = [], [], [], [], [], []
    for i in range(2):
        v_, r_ = view((112 + i) * KB, [128, 512], BF16)
        sq_v.append(v_); sq_b.append(abuf("sq%d" % i, r_))
        v_, r_ = view((114 + 2 * i) * KB, [128, 512], F32)
        lnt_v.append(v_); lnt_b.append(abuf("lnt%d" % i, r_))
        v_, r_ = view((118 + 2 * i) * KB, [128, 512], F32)
        rin_v.append(v_); rin_b.append(abuf("rin%d" % i, r_))
    SM_C64 = 250
    b_c64 = P.buf("c64")
    op("dve", lambda e: e.memset(small[:, SM_C64:SM_C64 + 1], 64.0 * EPS), [], [b_c64])
    QK_FAM = [(q_v, q_b, V_GQ), (k_v, k_b, V_GK)]

    def qk_unit(n_qk, f_, c, tt):
        dst_v, dst_b, gcol = QK_FAM[f_]
        wv, wb = wsl[c][f_]
        bq = 3 + n_qk % 2
        bs = 5 + n_qk % 2
        ti = n_qk % 2
        proj_fm(bq, wv, wb, 0, tt)
        act(sq_v[ti], bank(bq), AF.Square, [bankb[bq]], [sq_b[ti]])
        mm(bank(bs), BLK, sq_v[ti], True, True, [b_cbf, sq_b[ti]], [bankb[bs]])
        act(lnt_v[ti], bank(bs), AF.Ln, [bankb[bs], b_c64], [lnt_b[ti]], bias=small[:, SM_C64:SM_C64 + 1])
        act(rin_v[ti], lnt_v[ti], AF.Exp, [lnt_b[ti]], [rin_b[ti]], scale=-0.5)
        dst = dst_v[:, c, tt * 512:(tt + 1) * 512]
        op("dve", lambda e: e.scalar_tensor_tensor(out=dst, in0=bank(bq), scalar=vecs[:, gcol:gcol + 1],
                                                   in1=rin_v[ti], op0=ALU.mult, op1=ALU.mult),
           [bankb[bq], b_vecs, rin_b[ti]], [dst_b[c]])

    qk_units = []
    n_qk = 0
    for f_ in range(2):
        for tt in range(NTT):
            qk_units.append(lambda n_qk=n_qk, f_=f_, tt=tt: qk_unit(n_qk, f_, 0, tt))
            n_qk += 1
    PB1 = (0, 1, 2, 7)
    n1_stats(0)
    n1_stats(1)
    n1_stats(2)
    for ci in range(4):
        ada_chunk(ci, with_dma=False)
    ada_finish(1)
    n1_transp(0, banks=PB1)
    n1_stats(3)
    n1_transp(1, banks=PB1)
    qk_units.pop(0)()
    n1_transp(2, banks=PB1)
    qk_units.pop(0)()
    n1_transp(3, banks=PB1)
    load_win(1, 1, after=[hT_b[3]])
    load_win(0, 0, after=[hT_b[3]])
    for c in (1, 2, 3):
        load_qk_slices(c, after=[hT_b[3]])
    while qk_units:
        qk_units.pop(0)()
    v_b = abuf("v", v_r)
    for tb in range(NTB):
        bk = tb % 3
        tt = tb // 4
        for kc in range(8):
            mm(bank(bk), hT_v[:, kc, tb * 128:(tb + 1) * 128], wch_v[2][:, kc, :], kc == 0, kc == 7, [hT_b[tt], wch_b[2]], [bankb[bk]])
        if tb % 2 == 0:
            act(v_v[:, tb, :], bank(bk), AF.Copy, [bankb[bk]], [v_b])
        else:
            op("dve", lambda e, tb=tb, bk=bk: e.tensor_copy(v_v[:, tb, :], bank(bk)), [bankb[bk]], [v_b])
    load_win(2, 2)
    if dbg:
        dbg_out("q", [128, 4, S], BF16); dbg_out("k", [128, 4, S], BF16)
        for c in range(4):
            dma("sp", dbg_d["q"][:, c, :], q_v[:, c, :], reads=q_b)
            dma("sp", dbg_d["k"][:, c, :], k_v[:, c, :], reads=k_b)
        dbg_out("v", [128, NTB, 512], BF16)
        for c in range(4):
            dma("sp", dbg_d["v"][:, 4 * c:4 * c + 4, :], v_v[:, 4 * c:4 * c + 4, :], reads=[v_b])
    if stage <= 2:
        return finish()

    SB5, SB6 = 5, 5
    ada_bank[0] = SB5
    yc_v, yc_r = view(112 * KB, [128, 4, S], BF16)
    yc_b = abuf("yc", yc_r)
    csb_v, csb_r = view(184 * KB, [128, 512], F32)
    cu_v, cu_r = view(186 * KB, [128, 2 + 512], F32)
    acc_v, acc_r = view(186 * KB + 2304, [128, 512], F32)
    ysq_v, ysq_r = view(186 * KB + 2304 + 2048, [128, 512], BF16)
    csb_b, cu_b, acc_b, ysq_b = abuf("csb", csb_r), abuf("cu", cu_r), abuf("acc", acc_r), abuf("ysq", ysq_r)
    cw = lambda j, k_: vecs[:, V_CONV + 3 * j + k_:V_CONV + 3 * j + k_ + 1]

    def conv_a1_pe(j, tt, kcs):
        proj_fm(SB5, wch_v[1], wch_b[1], j * 128, tt, kcs)

    def conv_a1_dve(j, tt):
        op("dve", lambda e: e.tensor_copy(csb_v, bank(SB5)), [bankb[SB5]], [csb_b])

    def conv_a2_pe(j, tt, kcs):
        proj_fm(SB6, wch_v[2], wch_b[2], j * 128, tt, kcs)

    def conv_a2_dve(j, tt):
        if tt == 0:
            op("dve", lambda e: e.memset(cu_v[:, 0:2], 0.0), [], [cu_b])
        else:
            op("dve", lambda e: e.tensor_copy(cu_v[:, 0:2], cu_v[:, 512:514]), [cu_b], [cu_b])
        op("dve", lambda e: e.tensor_tensor(out=cu_v[:, 2:514], in0=bank(SB6), in1=csb_v, op=ALU.mult), [bankb[SB6], csb_b], [cu_b])

    def conv_b_pe(j, tt, kcs):
        proj_fm(SB5, wch_v[0], wch_b[0], j * 128, tt, kcs)

    def conv_b_dve1(j, tt):
        op("dve", lambda e: e.tensor_scalar_mul(out=acc_v, in0=cu_v[:, 2:514], scalar1=cw(j, 2)), [cu_b, b_vecs], [acc_b])
        op("dve", lambda e: e.scalar_tensor_tensor(out=acc_v, in0=cu_v[:, 1:513], scalar=cw(j, 1), in1=acc_v, op0=ALU.mult, op1=ALU.add),
           [cu_b, b_vecs, acc_b], [acc_b])
        op("dve", lambda e: e.scalar_tensor_tensor(out=acc_v, in0=cu_v[:, 0:512], scalar=cw(j, 0), in1=acc_v, op0=ALU.mult, op1=ALU.add),
           [cu_b, b_vecs, acc_b], [acc_b])

    def conv_b_dve2(j, tt):
        ydst = yc_v[:, j, tt * 512:(tt + 1) * 512]
        op("dve", lambda e: e.tensor_tensor(out=ydst, in0=bank(SB5), in1=acc_v, op=ALU.mult), [bankb[SB5], acc_b], [yc_b])
        op("dve", lambda e: e.tensor_tensor(out=ysq_v, in0=ydst, in1=ydst, op=ALU.mult), [yc_b], [ysq_b])

    def conv_b_stat(j, tt):
        for tbi in range(4):
            stat_mm(4 * tt + tbi, ysq_v[:, tbi * 128:(tbi + 1) * 128], [ysq_b])

    wo32_holder = {}

    def wo32_task():
        wo32_v, wo32_r = view(0, [128, 8, D], F32)
        wo32_b = abuf("wo32", wo32_r)
        for kc in range(8):
            dma("sp", wo32_v[:, kc, :], wout_d[kc * 128:(kc + 1) * 128, :], writes=[wo32_b])
        wo32_holder["v"], wo32_holder["b"] = wo32_v, wo32_b

    side = []
    ada_left = list(range(4, 12))
    nop = lambda: None
    for j in range(4):
        for tt in range(NTT):
            h0, h1 = range(0, 4), range(4, 8)
            side.append((lambda j=j, tt=tt: conv_a1_pe(j, tt, h0), nop))
            side.append((lambda j=j, tt=tt: conv_a1_pe(j, tt, h1), lambda j=j, tt=tt: conv_a1_dve(j, tt)))
            side.append((lambda j=j, tt=tt: conv_a2_pe(j, tt, h0), nop))
            side.append((lambda j=j, tt=tt: conv_a2_pe(j, tt, h1), lambda j=j, tt=tt: conv_a2_dve(j, tt)))
            side.append((lambda j=j, tt=tt: conv_b_pe(j, tt, h0), lambda j=j, tt=tt: conv_b_dve1(j, tt)))
            side.append((lambda j=j, tt=tt: conv_b_pe(j, tt, h1), lambda j=j, tt=tt: conv_b_dve2(j, tt)))
            if tt % 2 == 1 and ada_left:
                ci = ada_left.pop(0)
                side.append((lambda j=j, tt=tt, ci=ci: (conv_b_stat(j, tt), ada_chunk(ci, with_dma=False, part=0)), nop))
                side.append((lambda ci=ci: ada_chunk(ci, with_dma=False, part=1), nop))
            else:
                if tt % 2 == 0 and ada_left:
                    side.append((lambda j=j, tt=tt, ci=ada_left[0]: (conv_b_stat(j, tt), ada_dma(ci)), nop))
                else:
                    side.append((lambda j=j, tt=tt: conv_b_stat(j, tt), nop))
    side.append((lambda: ada_finish(2), nop))

    QB = 6
    sqh_v, sqh_r = view(61 * KB, [128, 256], BF16)
    pqs_v, pqs_r = view(61 * KB + 512, [128, 256], F32)
    lnh_v, lnh_r = view(62 * KB + 512, [128, 256], F32)
    sqh_b, pqs_b, lnh_b = abuf("sqh", sqh_r), abuf("pqs", pqs_r), abuf("lnh", lnh_r)
    pqh = bank(QB)[:, 0:256]
    psh = bank(QB)[:, 256:512]

    def qkh_e1_pe(f_, c, ht):
        wv, wb = wsl[c][f_]
        for kc in range(8):
            mm(pqh, wv[:, kc, :], hT_v[:, kc, ht * 256:(ht + 1) * 256], kc == 0, kc == 7, [wb, hT_b[ht // 2]], [bankb[QB]])

    def qkh_e1_def(f_, c, ht):
        op("dve", lambda e: e.tensor_copy(pqs_v, pqh), [bankb[QB]], [pqs_b])
        op("dve", lambda e: e.tensor_tensor(out=sqh_v, in0=pqs_v, in1=pqs_v, op=ALU.mult), [pqs_b], [sqh_b])

    def qkh_e2_pe(f_, c, ht):
        mm(psh, BLK, sqh_v, True, True, [b_cbf, sqh_b], [bankb[QB]], skip_group_check=True)

    def qkh_e2_def(f_, c, ht):
        dst_v, dst_b, gcol = QK_FAM[f_]
        act(lnh_v, psh, AF.Ln, [bankb[QB], b_c64], [lnh_b], bias=small[:, SM_C64:SM_C64 + 1])
        act(lnh_v, lnh_v, AF.Exp, [lnh_b], [lnh_b], scale=-0.5)
        dst = dst_v[:, c, ht * 256:(ht + 1) * 256]
        op("dve", lambda e: e.scalar_tensor_tensor(out=dst, in0=pqs_v, scalar=vecs[:, gcol:gcol + 1], in1=lnh_v, op0=ALU.mult, op1=ALU.mult),
           [pqs_b, b_vecs, lnh_b], [dst_b[c]])

    side_qk = []
    for c in (1, 2, 3):
        for f_ in range(2):
            for ht in range(8):
                side_qk.append((lambda f_=f_, c=c, ht=ht: qkh_e1_pe(f_, c, ht), lambda f_=f_, c=c, ht=ht: qkh_e1_def(f_, c, ht)))
                side_qk.append((lambda f_=f_, c=c, ht=ht: qkh_e2_pe(f_, c, ht), lambda f_=f_, c=c, ht=ht: qkh_e2_def(f_, c, ht)))

    ya_v, ya_r = view(128 * KB, [128, 4, S], BF16)
    ya_bd = {}

    def ya_buf(c):
        if c not in ya_bd:
            ya_bd[c] = abuf("ya%d" % c, (128 * KB + c * 4 * KB, 128 * KB + (c + 1) * 4 * KB))
        return ya_bd[c]
    u_v, u_b, sp_v, sp_b, a_v, a_b = [], [], [], [], [], []
    for i in range(3):
        v_, r_ = view((160 + 4 * i) * KB, [128, 2, 512], F32)
        u_v.append(v_); u_b.append(abuf("u%d" % i, r_))
    for i in range(2):
        v_, r_ = view((172 + 2 * i) * KB, [128, 2, 512], BF16)
        sp_v.append(v_); sp_b.append(abuf("sp%d" % i, r_))
        v_, r_ = view((176 + 2 * i) * KB, [128, 2, 512], BF16)
        a_v.append(v_); a_b.append(abuf("a%d" % i, r_))
    r_v, r_r = view(180 * KB, [128, 2, 512], F32)
    r_b = abuf("r", r_r)
    yasq_v, yasq_r = view(60 * KB, [128, 512], BF16)
    yasq_b = abuf("yasq", yasq_r)
    ZP = pair(0)
    ZPb = [bankb[0], bankb[1]]
    BP = pair(1)
    BPb = [bankb[2], bankb[3]]
    OBS = [4, 4]
    NEGM = cbf[:, C_NEGM:C_NEGM + 128]

    units = []
    for c in range(4):
        for i in range(NTT):
            kbs = list(range(4 * i + 3, -1, -1))
            for n_, kb in enumerate(kbs):
                units.append(dict(c=c, i=i, kb=kb, first=(n_ == 0), last=(kb == 0), qlo=max(0, kb - 4 * i) * 128, diag=(kb >= 4 * i),
                                  ob=OBS[(c * NTT + i) % 2]))
    NU = len(units)

    def stA(n):
        U = units[n]
        c, i, kb, qlo = U["c"], U["i"], U["kb"], U["qlo"]
        for h in range(2):
            mm(ZP[:, h, qlo:512], k_v[64 * h:64 * h + 64, c, kb * 128:(kb + 1) * 128], q_v[64 * h:64 * h + 64, c, i * 512 + qlo:(i + 1) * 512],
               True, not U["diag"], [k_b[c], q_b[c]], [ZPb[h]], tile_position=(64 * h, 0), skip_group_check=True)
            if U["diag"]:
                mm(ZP[:, h, qlo:qlo + 128], ident, NEGM, False, True, [b_cbf], [ZPb[h]], skip_group_check=True)

    def stB1(n):
        U = units[n]
        qlo = U["qlo"]
        act(u_v[n % 3][:, :, qlo:512], ZP[:, :, qlo:512], AF.Exp, ZPb, [u_b[n % 3]], scale=8.0)

    def stB2(n):
        U = units[n]
        qlo = U["qlo"]
        act(sp_v[n % 2][:, :, qlo:512], u_v[n % 3][:, :, qlo:512], AF.Ln, [u_b[n % 3]], [sp_b[n % 2]], bias=1.0)

    def stC_pe(n):
        U = units[n]
        qlo = U["qlo"]
        for h in range(2):
            mm(BP[:, h, qlo:512], TRI, sp_v[n % 2][:, h, qlo:512], U["first"], False, [b_cbf, sp_b[n % 2]], [BPb[h]], skip_group_check=True)

    def stC_act(n):
        U = units[n]
        qlo = U["qlo"]
        act(r_v[:, :, qlo:512], BP[:, :, qlo:512], AF.Exp, BPb, [r_b], scale=-1.0)

    def stD(n):
        U = units[n]
        qlo = U["qlo"]
        if not U["last"]:
            for h in range(2):
                mm(BP[:, h, qlo:512], COMP, sp_v[n % 2][:, h, qlo:512], False, False, [b_cbf, sp_b[n % 2]], [BPb[h]], skip_group_check=True)
        op("dve", lambda e, n=n, qlo=qlo: e.tensor_tensor(out=a_v[n % 2][:, :, qlo:512], in0=u_v[n % 3][:, :, qlo:512], in1=r_v[:, :, qlo:512], op=ALU.mult),
           [u_b[n % 3], r_b], [a_b[n % 2]])

    def stE(n):
        U = units[n]
        c, i, kb, qlo = U["c"], U["i"], U["kb"], U["qlo"]
        OB = U["ob"]
        for h in range(2):
            hh = 2 * c + h
            mm(bank(OB)[64 * h:64 * h + 64, qlo:512], v_v[:, kb, hh * 64:(hh + 1) * 64], a_v[n % 2][:, h, qlo:512], U["first"], U["last"],
               [v_b, a_b[n % 2]], [bankb[OB]], tile_position=(0, 64 * h), skip_group_check=True)
        if U["last"]:
            dst = ya_v[:, c, i * 512:(i + 1) * 512]
            op("dve", lambda e, dst=dst: e.tensor_copy(dst, bank(OB)), [bankb[OB]], [ya_buf(c)])
            op("dve", lambda e, dst=dst: e.tensor_tensor(out=yasq_v, in0=dst, in1=dst, op=ALU.mult), [ya_buf(c)], [yasq_b])
            pe_later.append([3, lambda i=i: [stat_mm(16 + 4 * i + tbi, yasq_v[:, tbi * 128:(tbi + 1) * 128], [yasq_b]) for tbi in range(4)]])

    pe_later = []

    def junk_mm(k):
        for _ in range(k):
            mm(bank(PSTAT)[:, 64:512], ident, cbf[:, 0:448], False, False, [b_cbf], [bankb[PSTAT]], skip_group_check=True)

    deferred = [[]]
    deferred_late = [[]]
    for step in range(NU + 4):
        for (fn, lag) in ((stD, 3), (stC_pe, 2), (stE, 4)):
            n = step - lag
            if 0 <= n < NU:
                fn(n)
        for ent in list(pe_later):
            if ent[0] == 0:
                ent[1]()
                pe_later.remove(ent)
            else:
                ent[0] -= 1
        for f in deferred.pop(0):
            f()
        if 0 <= step - 1 < NU:
            stB1(step - 1)
        for f in deferred_late.pop(0):
            f()
        for (fn, lag) in ((stC_act, 2), (stB2, 1)):
            n = step - lag
            if 0 <= n < NU:
                fn(n)
        if 0 <= step < NU:
            stA(step)
        late, late2 = [], []
        if step >= 1:
            if side:
                pe_part, late_part = side.pop(0)
                pe_part()
                late.append(late_part)
            if side_qk and step % 4 != 3:
                pe_part, late_part = side_qk.pop(0)
                pe_part()
                late2.append(late_part)
        if not late and not late2:
            if not wo32_holder and not side and not side_qk:
                wo32_task()
            junk_mm(NJUNK)
        deferred.append(late)
        deferred_late.append(late2)
    for f in deferred_late.pop(0):
        f()
    for f in deferred.pop(0):
        f()
    assert not side_qk
    for ent in pe_later:
        ent[1]()
    while side:
        pe_part, late_part = side.pop(0)
        pe_part()
        late_part()
    if not wo32_holder:
        wo32_task()
    ya_b = [ya_buf(c) for c in range(4)]
    wo32_v, wo32_b = wo32_holder["v"], wo32_holder["b"]
    if dbg:
        dbg_out("yc", [128, 4, S], BF16)
        dbg_out("ya", [128, 4, S], BF16)
        for c in range(4):
            dma("sp", dbg_d["yc"][:, c, :], yc_v[:, c, :], reads=[yc_b])
            dma("sp", dbg_d["ya"][:, c, :], ya_v[:, c, :], reads=ya_b)
    if stage <= 3:
        return finish()

    ffw_off = [144 * KB, 112 * KB]

    def ffn_weight_views(slot):
        w1v, w1r = view(ffw_off[slot], [128, 8, 1024], BF16)
        w2v, w2r = view(ffw_off[slot] + 16 * KB, [128, 8, 1024], BF16)
        return w1v, w1r, w2v, w2r

    def load_ffn_weights(qtr, slot):
        w1v, w1r, w2v, w2r = ffn_weight_views(slot)
        b1, b2 = abuf("w1q%d" % qtr, w1r), abuf("w2q%d" % qtr, w2r)
        for hh in range(2):
            dma("pool", w1v[:, 4 * hh:4 * hh + 4, :],
                wff1_d[512 * hh:512 * hh + 512, qtr * 1024:(qtr + 1) * 1024].rearrange("(kc p) n -> p kc n", p=128), writes=[b1])
        for hh in range(2):
            dma("pool", w2v[:, 4 * hh:4 * hh + 4, :],
                wff2_d[qtr * 1024 + 512 * hh:qtr * 1024 + 512 * hh + 512, :].rearrange("(kc p) n -> p kc n", p=128), writes=[b2])
        return w1v, b1, w2v, b2

    ffw = {0: load_ffn_weights(0, 0)}

    act(small[:, SM_TMP2:SM_TMP2 + 32], bank(PSTAT)[:, 0:32], AF.Ln, [bankb[PSTAT], b_eps], [b_small["tmp2"]],
        scale=1.0 / 512, bias=small[:, SM_EPS:SM_EPS + 1])
    act(small[:, SM_RSM:SM_RSM + 32], small[:, SM_TMP2:SM_TMP2 + 32], AF.Exp, [b_small["tmp2"]], [b_small["rsm"]], scale=-0.5)
    wog_v, wog_r = view(96 * KB, [128, 8, D], BF16)
    wog_b = abufs(["wog%d" % kc for kc in range(8)], wog_r)
    for kc in range(8):
        op("dve", lambda e, kc=kc: e.scalar_tensor_tensor(out=wog_v[:, kc, :], in0=wo32_v[:, kc, :], scalar=vecs[:, V_GMIX + kc:V_GMIX + kc + 1],
                                                          in1=g1b_v, op0=ALU.mult, op1=ALU.mult), [wo32_b, b_vecs, b_g1b], [wog_b[kc]])
    xs2_v, xs2_b = [], []
    for i in range(2):
        v_, r_ = view((176 + 4 * i) * KB, [128, D], F32)
        xs2_v.append(v_)
        xs2_b.append(abuf("xs2_%d" % i, r_))
    x1_v, x1_r = view(0, [128, NTB, D], F32)
    x1_b = [abuf("x1_%d" % tb, (tb * 4 * KB, (tb + 1) * 4 * KB)) for tb in range(NTB)]
    junk2_v, junk2_r = view(184 * KB, [128, D], BF16)
    junk2_b = abuf("junk2", junk2_r)
    _n2s, _n2t = make_norm(2, None, None, None, [None], [None], None)
    for tb in range(NTB):
        i = tb % 2
        dma("sp", xs2_v[i], x_d[tb * 128:(tb + 1) * 128, :], writes=[xs2_b[i]])
        for fh in range(2):
            bc, ba = (0, 1) if fh == 0 else (2, 3)
            for j in range(4):
                mm(bank(bc), yc_v[:, j, tb * 128:(tb + 1) * 128], wog_v[:, j, fh * 512:(fh + 1) * 512], j == 0, j == 3, [yc_b, wog_b[j]], [bankb[bc]])
            for c in range(4):
                mm(bank(ba), ya_v[:, c, tb * 128:(tb + 1) * 128], wog_v[:, 4 + c, fh * 512:(fh + 1) * 512], c == 0, c == 3, ya_b + [wog_b[4 + c]], [bankb[ba]])
            dst = x1_v[:, tb, fh * 512:(fh + 1) * 512]
            op("dve", lambda e, dst=dst, bc=bc, tb=tb, i=i, fh=fh: e.scalar_tensor_tensor(
                out=dst, in0=bank(bc), scalar=small[:, SM_RSM + tb:SM_RSM + tb + 1], in1=xs2_v[i][:, fh * 512:(fh + 1) * 512],
                op0=ALU.mult, op1=ALU.add), [bankb[bc], b_small["rsm"], xs2_b[i]], [x1_b[tb]])
            op("dve", lambda e, dst=dst, ba=ba, tb=tb: e.scalar_tensor_tensor(
                out=dst, in0=bank(ba), scalar=small[:, SM_RSM + 16 + tb:SM_RSM + 16 + tb + 1], in1=dst,
                op0=ALU.mult, op1=ALU.add), [bankb[ba], b_small["rsm"], x1_b[tb]], [x1_b[tb]])
        _n2t.stats_act(tb, x1_v[:, tb, :], x1_b[tb], junk2_v, junk2_b)
    if dbg:
        dbg_out("x1", [128, NTB, D], F32)
        for tb in range(NTB):
            dma("sp", dbg_d["x1"][:, tb, :], x1_v[:, tb, :], reads=[x1_b[tb]])
    if stage <= 4:
        return finish()

    h2T_v, h2T_r = view(64 * KB, [128, 8, S], BF16)
    h2T_b = abufs(["h2T%d" % tg for tg in range(NTT)], h2T_r)
    xn2_v, xn2_r = view(176 * KB, [128, 4, D], BF16)
    xn2_b = abuf("xn2", xn2_r)
    ffw[1] = load_ffn_weights(1, 1)
    n2_stats, n2_transp = make_norm(2, h2T_v, h2T_b, lambda tb: (x1_v[:, tb, :], x1_b[tb]), [xn2_v], [xn2_b],
                                    lambda tg, tbi: (xn2_v[:, tbi, :], xn2_b))
    gT_v, gT_b = [], []
    for i in range(2):
        v_, r_ = view((96 + 8 * i) * KB, [128, 8, 512], BF16)
        gT_v.append(v_)
        gT_b.append(abuf("gT%d" % i, r_))
    r32_v, r32_b, tmp_v, tmp_b = [], [], [], []
    for i in range(2):
        v_, r_ = view((184 + 2 * i) * KB, [128, 512], F32)
        r32_v.append(v_); r32_b.append(abuf("r32_%d" % i, r_))
        v_, r_ = view((188 + 2 * i) * KB, [128, 512], F32)
        tmp_v.append(v_); tmp_b.append(abuf("tmpf_%d" % i, r_))
    cnt = dict(f1=0, f2=0)

    def ffn1(qtr, tt):
        w1v, b1, _, _ = ffw[qtr]
        g = (qtr * NTT + tt) % 2
        for ffc in range(8):
            bk = cnt["f1"] % 3
            ri = cnt["f1"] % 2
            cnt["f1"] += 1
            for kc in range(8):
                mm(bank(bk), w1v[:, kc, ffc * 128:(ffc + 1) * 128], h2T_v[:, kc, tt * 512:(tt + 1) * 512], kc == 0, kc == 7,
                   [b1, h2T_b[tt]], [bankb[bk]])
            act(r32_v[ri], bank(bk), AF.Relu, [bankb[bk]], [r32_b[ri]])
            op("dve", lambda e, g=g, ffc=ffc, ri=ri: e.tensor_tensor(out=gT_v[g][:, ffc, :], in0=r32_v[ri], in1=r32_v[ri], op=ALU.mult),
               [r32_b[ri]], [gT_b[g]])

    def ffn2(qtr, tt):
        _, _, w2v, b2 = ffw[qtr]
        g = (qtr * NTT + tt) % 2
        for tbi in range(4):
            tb = 4 * tt + tbi
            for fh in range(2):
                bk = 3 + cnt["f2"] % 3
                ti = cnt["f2"] % 2
                cnt["f2"] += 1
                for ffc in range(8):
                    mm(bank(bk), gT_v[g][:, ffc, tbi * 128:(tbi + 1) * 128], w2v[:, ffc, fh * 512:(fh + 1) * 512], ffc == 0, ffc == 7,
                       [gT_b[g], b2], [bankb[bk]])
                op("dve", lambda e, bk=bk, ti=ti, fh=fh: e.tensor_tensor(out=tmp_v[ti], in0=bank(bk), in1=gate2_b[:, fh * 512:(fh + 1) * 512], op=ALU.mult),
                   [bankb[bk], b_g2b], [tmp_b[ti]])
                dst = x1_v[:, tb, fh * 512:(fh + 1) * 512]
                op(ADD_ENG, lambda e, dst=dst, ti=ti: e.tensor_tensor(out=dst, in0=dst, in1=tmp_v[ti], op=ALU.add), [x1_b[tb], tmp_b[ti]], [x1_b[tb]])
            if qtr == 3:
                dma("sp", out_d[tb * 128:(tb + 1) * 128, :], x1_v[:, tb, :], reads=[x1_b[tb]])

    steps = [(qtr, tt) for qtr in range(4) for tt in range(NTT)]
    n2_stats(0, skip_act=True)
    n2_transp(0, banks=(6, 7))
    n2_stats(1, skip_act=True)
    for s_, (qtr, tt) in enumerate(steps):
        ffn1(qtr, tt)
        if qtr == 0 and tt + 1 < NTT:
            n2_transp(tt + 1, banks=(6, 7))
            if tt + 2 < NTT:
                n2_stats(tt + 2, skip_act=True)
        if s_ >= 1:
            pq, pt = steps[s_ - 1]
            ffn2(pq, pt)
            if pt == NTT - 1 and pq + 2 < 4:
                ffw[pq + 2] = load_ffn_weights(pq + 2, pq % 2)
    ffn2(*steps[-1])

    return finish()


def prep_inputs(inputs):
    f = lambda a: np.ascontiguousarray(np.asarray(a, dtype=np.float32))
    x = f(inputs["x"])
    c = f(inputs["c"])
    b_ada = f(inputs["b_ada"])
    conv_w = f(inputs["conv_w"])
    shared = {
        "b_ada": b_ada,
        "w_ada": f(inputs["w_ada"]),
        "w_in": f(inputs["w_in"]),
        "w_out": f(inputs["w_out"]),
        "w_ff1": f(inputs["w_ff1"]),
        "w_ff2": f(inputs["w_ff2"]),
    }
    cb = np.zeros((128, 768), np.float32)
    ar = np.arange(128)
    cb[:, C_ID:C_ID + 128] = np.eye(128)
    cb[:, C_TRI:C_TRI + 128] = (ar[:, None] >= ar[None, :])
    cb[:, C_COMP:C_COMP + 128] = (ar[:, None] < ar[None, :])
    cb[:, C_BLK:C_BLK + 128] = ((ar[:, None] // 64) == (ar[None, :] // 64))
    cb[:, C_ONES:C_ONES + 128] = 1.0
    cb[:, C_NEGM:C_NEGM + 128] = np.where(ar[:, None] >= ar[None, :], -30000.0, 0.0)
    shared["cbf"] = cb.astype(ml_dtypes.bfloat16)
    shared["mask"] = (ar[:, None] < ar[None, :]).astype(np.float32)
    col = lambda v, n: np.asarray(v, np.float32).reshape(n, 128).T
    in_maps = []
    for b in range(x.shape[0]):
        vecs = np.zeros((128, NV), np.float32)
        vecs[:, V_C:V_C + 8] = col(c[b], 8)
        vecs[:, V_BADA:V_BADA + 48] = col(b_ada, 48)
        vecs[:, V_G1:V_G1 + 8] = col(inputs["norm1_g"], 8)
        vecs[:, V_G2:V_G2 + 8] = col(inputs["norm2_g"], 8)
        vecs[:, V_CONV:V_CONV + 12] = conv_w.reshape(3, 4, 128).transpose(2, 1, 0).reshape(128, 12)
        vecs[:, V_GQ] = np.tile(f(inputs["q_norm_g"]), 2)
        vecs[:, V_GK] = np.tile(f(inputs["k_norm_g"]), 2)
        vecs[:, V_GMIX:V_GMIX + 8] = col(np.concatenate([f(inputs["conv_out_g"]), f(inputs["attn_out_g"])]), 8)
        m = dict(shared)
        m["x"] = x[b]
        m["vecs"] = vecs
        in_maps.append(m)
    return in_maps


_CACHE = {}


def kernel(**inputs):
    in_maps = prep_inputs(inputs)
    if "nc" not in _CACHE:
        nc = bass.Bass("TRN2", target_bir_lowering=False)
        build(nc)
        _CACHE["nc"] = nc
    nc = _CACHE["nc"]
    res = run_bass_kernel_spmd(nc, in_maps, core_ids=list(range(8)))
    out = np.stack([np.asarray(r["out"], dtype=np.float32) for r in res.results], axis=0)
    return out
```

```python
import numpy as np
import ml_dtypes
import concourse.bass as bass
import concourse.mybir as mybir
from concourse.bass_utils import run_bass_kernel_spmd

F32 = mybir.dt.float32
BF16 = mybir.dt.bfloat16
AF = mybir.ActivationFunctionType
ALU = mybir.AluOpType

ENGINES = ("pe", "act", "dve", "pool", "sp")
STRICT_SAME_ENGINE = True


class Buf:
    __slots__ = ("name", "last_writer", "readers", "dead")

    def __init__(self, name):
        self.name = name
        self.last_writer = None
        self.readers = []
        self.dead = False


class Op:
    __slots__ = ("idx", "eng", "fn", "is_dma", "deps", "has_dep", "seq", "dsem", "dval", "name")

    def __init__(self, idx, eng, fn, is_dma, name):
        self.idx = idx
        self.eng = eng
        self.fn = fn
        self.is_dma = is_dma
        self.deps = []
        self.has_dep = False
        self.seq = None
        self.dsem = None
        self.dval = None
        self.name = name


class Prog:
    NDMA_SEMS = 8

    def __init__(self):
        self.ops = []

    def buf(self, name):
        return Buf(name)

    def add(self, eng, fn, reads=(), writes=(), dma=False, name=""):
        op = Op(len(self.ops), eng, fn, dma, name)
        deps = {}
        for b in reads:
            w = b.last_writer
            if w is not None:
                deps[w.idx] = (w, True)
        for b in writes:
            w = b.last_writer
            if w is not None and w.idx not in deps:
                deps[w.idx] = (w, False)
            for r in b.readers:
                if r.idx not in deps:
                    deps[r.idx] = (r, False)
        for b in reads:
            b.readers.append(op)
        for b in writes:
            b.last_writer = op
            b.readers = []
        for d, raw in deps.values():
            if d is op:
                continue
            if (not d.is_dma) and d.eng == eng and not raw and not (STRICT_SAME_ENGINE and eng != "pe"):
                continue
            op.deps.append(d)
            d.has_dep = True
        self.ops.append(op)
        return op

    def alias(self, new_bufs, old_bufs):
        olds = []
        for b in old_bufs:
            if b.last_writer is not None:
                olds.append(b.last_writer)
            olds.extend(b.readers)
        for nb in new_bufs:
            nb.readers = list(olds) + nb.readers

    def emit(self, nc, block, engmap):
        per_eng = {e: [] for e in ENGINES}
        for op in self.ops:
            per_eng[op.eng].append(op)
        counters = {e: 0 for e in ENGINES}
        dma_count = {e: 0 for e in ENGINES}
        for op in self.ops:
            if op.is_dma:
                k = dma_count[op.eng]
                dma_count[op.eng] += 1
                op.dsem = (op.eng, k % self.NDMA_SEMS)
                op.dval = 16 * (k // self.NDMA_SEMS + 1)
            elif op.has_dep:
                counters[op.eng] += 1
                op.seq = counters[op.eng]
        self.stats = {e: len(per_eng[e]) for e in ENGINES}

        csem = self.csem
        dsems = self.dsems

        def run_engine(ename):
            def body(eng):
                waited = {}
                last_dma_on_sem = {}

                def wait(key, sem, val):
                    if waited.get(key, 0) >= val:
                        return
                    waited[key] = val
                    eng.wait_ge(sem, val)

                for op in per_eng[ename]:
                    for d in op.deps:
                        if d.is_dma:
                            wait(("d",) + d.dsem, dsems[d.dsem[0]][d.dsem[1]], d.dval)
                        else:
                            wait(("c", d.eng), csem[d.eng], d.seq)
                    if op.is_dma:
                        if op.dval > 16:
                            wait(("d",) + op.dsem, dsems[op.dsem[0]][op.dsem[1]], op.dval - 16)
                        ins = op.fn(eng)
                        ins.then_inc(dsems[op.dsem[0]][op.dsem[1]], 16)
                        last_dma_on_sem[op.dsem] = op.dval
                    else:
                        ins = op.fn(eng)
                        if op.seq is not None:
                            ins.then_inc(csem[ename], 1)
                for key, val in last_dma_on_sem.items():
                    wait(("d",) + key, dsems[key[0]][key[1]], val)
            return body

        for ename in ENGINES:
            if not per_eng[ename]:
                continue
            getattr(block, engmap[ename])(run_engine(ename))


BLOCK_ATTR = {"pe": "tensor", "act": "scalar", "dve": "vector", "pool": "gpsimd", "sp": "sync"}


def run_prog(nc, prog):
    from contextlib import ExitStack
    with ExitStack() as es:
        prog.csem = {e: es.enter_context(nc.semaphore("c_" + e)) for e in ENGINES}
        prog.dsems = {e: [es.enter_context(nc.semaphore("d_%s_%d" % (e, i))) for i in range(Prog.NDMA_SEMS)]
                      for e in ("sp", "act", "pool")}
        block = es.enter_context(nc.Block())
        prog.emit(nc, block, BLOCK_ATTR)


S = 2048
D = 1024
DIN = 3072
DFF = 4096
NTB = 16
NTT = 4
EPS = 1e-6
KB = 1024
NV = 96
V_C, V_BADA, V_G1, V_G2, V_CONV, V_GQ, V_GK, V_GMIX = 0, 8, 56, 64, 72, 84, 85, 86
C_ID, C_TRI, C_COMP, C_BLK, C_ONES, C_NEGM = 0, 128, 256, 384, 512, 640


ADD_ENG = "dve"
NJUNK = 4


def build(nc, dbg=False, stage=99):
    from contextlib import ExitStack
    P = Prog()
    di = lambda n, s, d: nc.dram_tensor(n, s, d, kind="ExternalInput").ap()
    x_d = di("x", [S, D], F32)
    vecs_d = di("vecs", [128, NV], F32)
    cbf_d = di("cbf", [128, 768], BF16)
    mask_d = di("mask", [128, 128], F32)
    bada_d = di("b_ada", [6 * D], F32)
    wada_d = di("w_ada", [D, 6 * D], F32)
    win_d = di("w_in", [D, DIN], F32)
    wout_d = di("w_out", [D, D], F32)
    wff1_d = di("w_ff1", [D, DFF], F32)
    wff2_d = di("w_ff2", [DFF, D], F32)
    out_d = nc.dram_tensor("out", [S, D], F32, kind="ExternalOutput").ap()
    dbg_d = {}

    def dbg_out(name, shape, dt):
        dbg_d[name] = nc.dram_tensor("dbg_" + name, shape, dt, kind="ExternalOutput").ap()
        return dbg_d[name]

    es = ExitStack()

    def finish():
        run_prog(nc, P)
        es.close()
        return P, dbg_d

    sb = lambda n, s, d: es.enter_context(nc.sbuf_tensor(n, s, d))
    AR = sb("arena", [128, 196 * KB // 2], BF16)
    cbf = sb("cbf_sb", [128, 768], BF16)
    mask = sb("mask_sb", [128, 128], F32)
    vecs = sb("vecs_sb", [128, NV], F32)
    small = sb("small", [128, 256], F32)
    scbf = sb("scbf", [128, 8], BF16)
    scb = sb("scb", [128, 8, 128], BF16)
    gate2_b = sb("gate2_b", [128, D], F32)
    PS = [es.enter_context(nc.psum_tensor("ps%d" % i, [128, 1024], F32)) for i in range(4)]

    def bank(b):
        return PS[b // 2][:, (b % 2) * 512:(b % 2) * 512 + 512]

    def bank_bf(b):
        return PS[b // 2][:, (b % 2) * 512:(b % 2) * 512 + 256].bitcast(BF16)

    def pair(k):
        return PS[k][:].rearrange("p (a b) -> p a b", b=512)

    bankb = [P.buf("bank%d" % i) for i in range(8)]

    SM_MODT = 0
    SM_A1, SM_S1, SM_A2, SM_S2 = 48, 56, 64, 72
    SM_SC32 = 80
    SM_SS1, SM_RS1 = 88, 104
    SM_SS2, SM_RS2 = 120, 136
    SM_RSM = 152
    SM_TMP = 184
    SM_TMP2 = 216
    b_small = {k: P.buf("sm_" + k) for k in ["modT", "A1S1", "A2S2", "sc", "ss1", "rs1", "ss2", "rs2", "rsm", "tmp", "tmp2"]}

    arena_bufs = []

    def view(off, shape, dt):
        n = 1
        for d_ in shape[1:]:
            n *= d_
        nbytes = n * (4 if dt == F32 else 2)
        a = AR[:, off // 2:(off + nbytes) // 2]
        if dt == F32:
            a = a.bitcast(F32)
        if len(shape) == 3:
            a = a.rearrange("p (a b) -> p a b", b=shape[2])
        return a, (off, off + nbytes)

    def abuf(name, rng):
        b = P.buf(name)
        olds = [ob for (lo, hi, ob) in arena_bufs if lo < rng[1] and rng[0] < hi]
        P.alias([b], olds)
        for ob in olds:
            ob.dead = True
        arena_bufs.append((rng[0], rng[1], b))
        return b

    def abufs(names, rng):
        olds = [ob for (lo, hi, ob) in arena_bufs if lo < rng[1] and rng[0] < hi]
        bs = []
        for nm in names:
            b = P.buf(nm)
            P.alias([b], olds)
            bs.append(b)
        for ob in olds:
            ob.dead = True
        for b in bs:
            arena_bufs.append((rng[0], rng[1], b))
        return bs

    def chk(bufs):
        for b in bufs:
            assert not getattr(b, "dead", False), "use of dead buffer " + b.name

    def op(eng, fn, reads=(), writes=(), dma=False, name=""):
        chk(reads)
        chk(writes)
        return P.add(eng, fn, reads=list(reads), writes=list(writes), dma=dma, name=name)

    def dma(eng, out, in_, reads=(), writes=()):
        return op(eng, lambda e: e.dma_start(out=out, in_=in_), reads, writes, dma=True)

    def mm(out, lhsT, rhs, start, stop, reads, writes, **kw):
        return op("pe", lambda e: e.matmul(out, lhsT=lhsT, rhs=rhs, start=start, stop=stop, **kw), reads, writes)

    def act(out, in_, func, reads, writes, **kw):
        return op("act", lambda e: e.activation(out=out, in_=in_, func=func, **kw), reads, writes)

    ident = cbf[:, C_ID:C_ID + 128]
    TRI = cbf[:, C_TRI:C_TRI + 128]
    COMP = cbf[:, C_COMP:C_COMP + 128]
    BLK = cbf[:, C_BLK:C_BLK + 128]
    ONES = cbf[:, C_ONES:C_ONES + 128]
    b_cbf, b_mask, b_vecs, b_scbf, b_scb, b_g2b = [P.buf(n) for n in ["cbf", "mask", "vecs", "scbf", "scb", "g2b"]]

    dma("sp", cbf[:], cbf_d, writes=[b_cbf])
    dma("sp", vecs[:], vecs_d, writes=[b_vecs])
    dma("sp", mask[:], mask_d, writes=[b_mask])
    g1b_v, g1b_r = view(56 * KB, [128, D], F32)
    b_g1b = abuf("gate1_b", g1b_r)
    dma("sp", g1b_v, bada_d[2 * D:3 * D].partition_broadcast(128), writes=[b_g1b])
    dma("sp", gate2_b[:], bada_d[5 * D:6 * D].partition_broadcast(128), writes=[b_g2b])

    sc32 = small[:, SM_SC32:SM_SC32 + 8]
    act(sc32, vecs[:, V_C:V_C + 8], AF.Silu, [b_vecs], [b_small["sc"]])
    op("dve", lambda e: e.tensor_copy(scbf[:], sc32), [b_small["sc"]], [b_scbf])
    for kc in range(8):
        op("dve", lambda e, kc=kc: e.tensor_scalar_mul(out=scb[:, kc, :], in0=ONES, scalar1=small[:, SM_SC32 + kc:SM_SC32 + kc + 1]),
           [b_small["sc"], b_cbf], [b_scb])
    SM_EPS = 251
    b_eps = P.buf("eps")
    op("dve", lambda e: e.memset(small[:, SM_EPS:SM_EPS + 1], EPS), [], [b_eps])
    op("dve", lambda e: e.memset(small[:, SM_SS1:SM_SS1 + 16], 0.0), [], [b_small["ss1"]])
    op("dve", lambda e: e.memset(small[:, SM_SS2:SM_SS2 + 16], 0.0), [], [b_small["ss2"]])

    wa_v, wa_b = [], []
    for off in (144, 152, 96, 104):
        v_, r_ = view(off * KB, [128, 8, 512], BF16)
        wa_v.append(v_)
        wa_b.append(abuf("wa%d" % len(wa_v), r_))
    wa_slot = lambda ci: ci if ci < 4 else ci % 2
    ada_bank = [7]

    def ada_dma(ci):
        wv, wb = wa_v[wa_slot(ci)], wa_b[wa_slot(ci)]
        dma("pool", wv, wada_d[:, ci * 512:(ci + 1) * 512].rearrange("(kc p) n -> p kc n", p=128), writes=[wb])

    def ada_chunk(ci, with_dma=True, part=None):
        wv, wb = wa_v[wa_slot(ci)], wa_b[wa_slot(ci)]
        PB = ada_bank[0]
        if with_dma:
            ada_dma(ci)
        kind = ci // 2
        if kind in (2, 5):
            half = ci % 2
            kcs = range(8) if part is None else range(4 * part, 4 * part + 4)
            for kc in kcs:
                mm(bank(PB), scb[:, kc, :], wv[:, kc, :], kc == 0, kc == 7, [b_scb, wb], [bankb[PB]])
            if part in (None, 1):
                gv, gb = (g1b_v, b_g1b) if kind == 2 else (gate2_b[:], b_g2b)
                op("dve", lambda e: e.tensor_tensor(out=gv[:, half * 512:(half + 1) * 512], in0=bank(PB), in1=gv[:, half * 512:(half + 1) * 512], op=ALU.add),
                   [bankb[PB], gb], [gb])
        else:
            jjs = range(4) if part is None else range(2 * part, 2 * part + 2)
            for jj in jjs:
                for kc in range(8):
                    mm(bank(PB)[:, jj:jj + 1], wv[:, kc, jj * 128:(jj + 1) * 128], scbf[:, kc:kc + 1], kc == 0, kc == 7,
                       [b_scbf, wb], [bankb[PB]], skip_group_check=True)
            if part in (None, 1):
                j0 = 4 * ci
                op("dve", lambda e: e.tensor_tensor(out=small[:, SM_MODT + j0:SM_MODT + j0 + 4], in0=bank(PB)[:, 0:4],
                                                    in1=vecs[:, V_BADA + j0:V_BADA + j0 + 4], op=ALU.add),
                   [bankb[PB], b_vecs], [b_small["modT"]])

    def ada_finish(which):
        base = 0 if which == 1 else 24
        A, Sx, G = (SM_A1, SM_S1, V_G1) if which == 1 else (SM_A2, SM_S2, V_G2)
        key = "A1S1" if which == 1 else "A2S2"
        op("dve", lambda e: e.scalar_tensor_tensor(out=small[:, A:A + 8], in0=small[:, SM_MODT + base + 8:SM_MODT + base + 16], scalar=1.0,
                                                   in1=vecs[:, G:G + 8], op0=ALU.add, op1=ALU.mult),
           [b_small["modT"], b_vecs], [b_small[key]])
        op("dve", lambda e: e.tensor_copy(small[:, Sx:Sx + 8], small[:, SM_MODT + base:SM_MODT + base + 8]),
           [b_small["modT"]], [b_small[key]])

    for ci in range(4):
        ada_dma(ci)

    def make_norm(which, hT_v, hT_b, get_block, xn_views, xn_bufs, junk_of):
        SS, RS = (SM_SS1, SM_RS1) if which == 1 else (SM_SS2, SM_RS2)
        kss, krs = ("ss1", "rs1") if which == 1 else ("ss2", "rs2")
        A, Sx = (SM_A1, SM_S1) if which == 1 else (SM_A2, SM_S2)
        kAS = "A1S1" if which == 1 else "A2S2"
        nb = len(xn_views)

        def stats_act(tb, xv, xb_, jv, jb):
            act(jv, xv, AF.Square, [xb_], [jb, b_small[kss]], accum_out=small[:, SS + tb:SS + tb + 1], scale=float(D) ** -0.5)
            act(small[:, SM_TMP2 + tb:SM_TMP2 + tb + 1], small[:, SS + tb:SS + tb + 1], AF.Ln, [b_small[kss], b_eps], [b_small["tmp2"]],
                bias=small[:, SM_EPS:SM_EPS + 1])
            act(small[:, RS + tb:RS + tb + 1], small[:, SM_TMP2 + tb:SM_TMP2 + tb + 1], AF.Exp, [b_small["tmp2"]], [b_small[krs]], scale=-0.5)

        def stats(tg, skip_act=False):
            xnv, xnb = xn_views[tg % nb], xn_bufs[tg % nb]
            for tbi in range(4):
                tb = 4 * tg + tbi
                xv, xb_ = get_block(tb)
                if not skip_act:
                    jv, jb = junk_of(tg, tbi)
                    stats_act(tb, xv, xb_, jv, jb)
                op("dve", lambda e, tbi=tbi, tb=tb, xv=xv, xnv=xnv: e.tensor_scalar_mul(out=xnv[:, tbi, :], in0=xv, scalar1=small[:, RS + tb:RS + tb + 1]),
                   [xb_, b_small[krs]], [xnb])

        def transp(tg, banks=(0, 1)):
            xnv, xnb = xn_views[tg % nb], xn_bufs[tg % nb]
            for kc in range(8):
                bk = banks[kc % len(banks)]
                for tbi in range(4):
                    op("pe", lambda e, bk=bk, tbi=tbi, kc=kc, xnv=xnv: e.transpose(bank_bf(bk)[:, tbi * 128:(tbi + 1) * 128],
                                                                                 xnv[:, tbi, kc * 128:(kc + 1) * 128], ident),
                       [xnb, b_cbf], [bankb[bk]])
                dst = hT_v[:, kc, tg * 512:(tg + 1) * 512]
                if kc % 2 == 0:
                    act(dst, bank_bf(bk), AF.Identity, [bankb[bk], b_small[kAS]], [hT_b[tg]],
                        scale=small[:, A + kc:A + kc + 1], bias=small[:, Sx + kc:Sx + kc + 1])
                else:
                    op("dve", lambda e, dst=dst, bk=bk, kc=kc: e.tensor_scalar(out=dst, in0=bank_bf(bk), scalar1=small[:, A + kc:A + kc + 1],
                                                                             scalar2=small[:, Sx + kc:Sx + kc + 1], op0=ALU.mult, op1=ALU.add),
                       [bankb[bk], b_small[kAS]], [hT_b[tg]])
        transp.stats_act = stats_act
        return stats, transp

    hT_v, hT_r = view(0, [128, 8, S], BF16)
    hT_b = abufs(["hT%d" % tg for tg in range(NTT)], hT_r)
    xs_v, xs_b = [], []
    for i in range(4):
        v_, r_ = view((176 + 4 * i) * KB, [128, D], F32)
        xs_v.append(v_)
        xs_b.append(abuf("xs%d" % i, r_))
    junk_v, junk_r = view(60 * KB, [128, D], BF16)
    junk_b = abuf("junk", junk_r)
    xn_v, xn_b = [], []
    for i, off in enumerate((160, 168, 122)):
        v_, r_ = view(off * KB, [128, 4, D], BF16)
        xn_v.append(v_)
        xn_b.append(abuf("xn%d" % i, r_))

    def load_x_block(tb):
        i = tb % 4
        dma("sp", xs_v[i], x_d[tb * 128:(tb + 1) * 128, :], writes=[xs_b[i]])
        return xs_v[i], xs_b[i]

    n1_stats, n1_transp = make_norm(1, hT_v, hT_b, load_x_block, xn_v, xn_b, lambda tg, tbi: (junk_v, junk_b))
    wch_v, wch_b = [], []
    for i in range(3):
        v_, r_ = view((32 + 8 * i) * KB, [128, 8, 512], BF16)
        wch_v.append(v_)
        wch_b.append(abuf("wch%d" % i, r_))

    def load_win(fam, slot, after=()):
        dma("pool", wch_v[slot], win_d[:, fam * 512:(fam + 1) * 512].rearrange("(kc p) n -> p kc n", p=128), reads=list(after), writes=[wch_b[slot]])

    SL_OFF = {0: 132 * KB, 1: 136 * KB, 2: 140 * KB, 3: 192 * KB}
    wsl = {}

    def load_qk_slices(c, after=()):
        out = []
        for f_, foff in enumerate((1536, 2048)):
            v_, r_ = view(SL_OFF[c] + 2 * KB * f_, [128, 8, 128], BF16)
            b_ = abuf("wsl%d_%d" % (c, f_), r_)
            dma("pool", v_, win_d[:, foff + c * 128:foff + (c + 1) * 128].rearrange("(kc p) n -> p kc n", p=128), reads=list(after), writes=[b_])
            out.append((v_, b_))
        wsl[c] = out

    load_qk_slices(0)
    load_win(5, 2)

    q_v, q_r = view(64 * KB, [128, 4, S], BF16)
    k_v, k_r = view(80 * KB, [128, 4, S], BF16)
    v_v, v_r = view(96 * KB, [128, NTB, 512], BF16)
    q_b = [abuf("q%d" % c, (64 * KB + c * 4 * KB, 64 * KB + (c + 1) * 4 * KB)) for c in range(4)]
    k_b = [abuf("k%d" % c, (80 * KB + c * 4 * KB, 80 * KB + (c + 1) * 4 * KB)) for c in range(4)]
    PSTAT = 7
    stat_started = [False]

    def stat_mm(col, lhsT, reads):
        st = not stat_started[0]
        stat_started[0] = True
        mm(bank(PSTAT)[:, col:col + 1], lhsT, ONES[:, 0:1], st, False, list(reads) + [b_cbf], [bankb[PSTAT]], skip_group_check=True)

    def proj_fm(bk, wv, wb, col0, tt, kcs=range(8)):
        for kc in kcs:
            mm(bank(bk), wv[:, kc, col0:col0 + 128], hT_v[:, kc, tt * 512:(tt + 1) * 512], kc == 0, kc == 7,
               [wb, hT_b[tt]], [bankb[bk]])

    sq_v, sq_b, lnt_v, lnt_b, rin_v, rin_b = [], [], [], [], [], []
    for i in range(2):
        v_, r_ = view((112 + i) * KB, [128, 512], BF16)
        sq_v.append(v_); sq_b.append(abuf("sq%d" % i, r_))
        v_, r_ = view((114 + 2 * i) * KB, [128, 512], F32)
        lnt_v.append(v_); lnt_b.append(abuf("lnt%d" % i, r_))
        v_, r_ = view((118 + 2 * i) * KB, [128, 512], F32)
        rin_v.append(v_); rin_b.append(abuf("rin%d" % i, r_))
    SM_C64 = 250
    b_c64 = P.buf("c64")
    op("dve", lambda e: e.memset(small[:, SM_C64:SM_C64 + 1], 64.0 * EPS), [], [b_c64])
    QK_FAM = [(q_v, q_b, V_GQ), (k_v, k_b, V_GK)]

    def qk_unit(n_qk, f_, c, tt):
        dst_v, dst_b, gcol = QK_FAM[f_]
        wv, wb = wsl[c][f_]
        bq = 3 + n_qk % 2
        bs = 5 + n_qk % 2
        ti = n_qk % 2
        proj_fm(bq, wv, wb, 0, tt)
        act(sq_v[ti], bank(bq), AF.Square, [bankb[bq]], [sq_b[ti]])
        mm(bank(bs), BLK, sq_v[ti], True, True, [b_cbf, sq_b[ti]], [bankb[bs]])
        act(lnt_v[ti], bank(bs), AF.Ln, [bankb[bs], b_c64], [lnt_b[ti]], bias=small[:, SM_C64:SM_C64 + 1])
        act(rin_v[ti], lnt_v[ti], AF.Exp, [lnt_b[ti]], [rin_b[ti]], scale=-0.5)
        dst = dst_v[:, c, tt * 512:(tt + 1) * 512]
        op("dve", lambda e: e.scalar_tensor_tensor(out=dst, in0=bank(bq), scalar=vecs[:, gcol:gcol + 1],
                                                   in1=rin_v[ti], op0=ALU.mult, op1=ALU.mult),
           [bankb[bq], b_vecs, rin_b[ti]], [dst_b[c]])

    qk_units = []
    n_qk = 0
    for f_ in range(2):
        for tt in range(NTT):
            qk_units.append(lambda n_qk=n_qk, f_=f_, tt=tt: qk_unit(n_qk, f_, 0, tt))
            n_qk += 1
    PB1 = (0, 1, 2, 7)
    n1_stats(0)
    n1_stats(1)
    n1_stats(2)
    for ci in range(4):
        ada_chunk(ci, with_dma=False)
    ada_finish(1)
    n1_transp(0, banks=PB1)
    n1_stats(3)
    n1_transp(1, banks=PB1)
    qk_units.pop(0)()
    n1_transp(2, banks=PB1)
    qk_units.pop(0)()
    n1_transp(3, banks=PB1)
    load_win(1, 1, after=[hT_b[3]])
    load_win(0, 0, after=[hT_b[3]])
    for c in (1, 2, 3):
        load_qk_slices(c, after=[hT_b[3]])
    while qk_units:
        qk_units.pop(0)()
    v_b = abuf("v", v_r)
    for tb in range(NTB):
        bk = tb % 3
        tt = tb // 4
        for kc in range(8):
            mm(bank(bk), hT_v[:, kc, tb * 128:(tb + 1) * 128], wch_v[2][:, kc, :], kc == 0, kc == 7, [hT_b[tt], wch_b[2]], [bankb[bk]])
        if tb % 2 == 0:
            act(v_v[:, tb, :], bank(bk), AF.Copy, [bankb[bk]], [v_b])
        else:
            op("dve", lambda e, tb=tb, bk=bk: e.tensor_copy(v_v[:, tb, :], bank(bk)), [bankb[bk]], [v_b])
    load_win(2, 2)
    if dbg:
        dbg_out("q", [128, 4, S], BF16); dbg_out("k", [128, 4, S], BF16)
        for c in range(4):
            dma("sp", dbg_d["q"][:, c, :], q_v[:, c, :], reads=q_b)
            dma("sp", dbg_d["k"][:, c, :], k_v[:, c, :], reads=k_b)
        dbg_out("v", [128, NTB, 512], BF16)
        for c in range(4):
            dma("sp", dbg_d["v"][:, 4 * c:4 * c + 4, :], v_v[:, 4 * c:4 * c + 4, :], reads=[v_b])
    if stage <= 2:
        return finish()

    SB5, SB6 = 5, 5
    ada_bank[0] = SB5
    yc_v, yc_r = view(112 * KB, [128, 4, S], BF16)
    yc_b = abuf("yc", yc_r)
    csb_v, csb_r = view(184 * KB, [128, 512], F32)
    cu_v, cu_r = view(186 * KB, [128, 2 + 512], F32)
    acc_v, acc_r = view(186 * KB + 2304, [128, 512], F32)
    ysq_v, ysq_r = view(186 * KB + 2304 + 2048, [128, 512], BF16)
    csb_b, cu_b, acc_b, ysq_b = abuf("csb", csb_r), abuf("cu", cu_r), abuf("acc", acc_r), abuf("ysq", ysq_r)
    cw = lambda j, k_: vecs[:, V_CONV + 3 * j + k_:V_CONV + 3 * j + k_ + 1]

    def conv_a1_pe(j, tt, kcs):
        proj_fm(SB5, wch_v[1], wch_b[1], j * 128, tt, kcs)

    def conv_a1_dve(j, tt):
        op("dve", lambda e: e.tensor_copy(csb_v, bank(SB5)), [bankb[SB5]], [csb_b])

    def conv_a2_pe(j, tt, kcs):
        proj_fm(SB6, wch_v[2], wch_b[2], j * 128, tt, kcs)

    def conv_a2_dve(j, tt):
        if tt == 0:
            op("dve", lambda e: e.memset(cu_v[:, 0:2], 0.0), [], [cu_b])
        else:
            op("dve", lambda e: e.tensor_copy(cu_v[:, 0:2], cu_v[:, 512:514]), [cu_b], [cu_b])
        op("dve", lambda e: e.tensor_tensor(out=cu_v[:, 2:514], in0=bank(SB6), in1=csb_v, op=ALU.mult), [bankb[SB6], csb_b], [cu_b])

    def conv_b_pe(j, tt, kcs):
        proj_fm(SB5, wch_v[0], wch_b[0], j * 128, tt, kcs)

    def conv_b_dve1(j, tt):
        op("dve", lambda e: e.tensor_scalar_mul(out=acc_v, in0=cu_v[:, 2:514], scalar1=cw(j, 2)), [cu_b, b_vecs], [acc_b])
        op("dve", lambda e: e.scalar_tensor_tensor(out=acc_v, in0=cu_v[:, 1:513], scalar=cw(j, 1), in1=acc_v, op0=ALU.mult, op1=ALU.add),
           [cu_b, b_vecs, acc_b], [acc_b])
        op("dve", lambda e: e.scalar_tensor_tensor(out=acc_v, in0=cu_v[:, 0:512], scalar=cw(j, 0), in1=acc_v, op0=ALU.mult, op1=ALU.add),
           [cu_b, b_vecs, acc_b], [acc_b])

    def conv_b_dve2(j, tt):
        ydst = yc_v[:, j, tt * 512:(tt + 1) * 512]
        op("dve", lambda e: e.tensor_tensor(out=ydst, in0=bank(SB5), in1=acc_v, op=ALU.mult), [bankb[SB5], acc_b], [yc_b])
        op("dve", lambda e: e.tensor_tensor(out=ysq_v, in0=ydst, in1=ydst, op=ALU.mult), [yc_b], [ysq_b])

    def conv_b_stat(j, tt):
        for tbi in range(4):
            stat_mm(4 * tt + tbi, ysq_v[:, tbi * 128:(tbi + 1) * 128], [ysq_b])

    wo32_holder = {}

    def wo32_task():
        wo32_v, wo32_r = view(0, [128, 8, D], F32)
        wo32_b = abuf("wo32", wo32_r)
        for kc in range(8):
            dma("sp", wo32_v[:, kc, :], wout_d[kc * 128:(kc + 1) * 128, :], writes=[wo32_b])
        wo32_holder["v"], wo32_holder["b"] = wo32_v, wo32_b

    side = []
    ada_left = list(range(4, 12))
    nop = lambda: None
    for j in range(4):
        for tt in range(NTT):
            h0, h1 = range(0, 4), range(4, 8)
            side.append((lambda j=j, tt=tt: conv_a1_pe(j, tt, h0), nop))
            side.append((lambda j=j, tt=tt: conv_a1_pe(j, tt, h1), lambda j=j, tt=tt: conv_a1_dve(j, tt)))
            side.append((lambda j=j, tt=tt: conv_a2_pe(j, tt, h0), nop))
            side.append((lambda j=j, tt=tt: conv_a2_pe(j, tt, h1), lambda j=j, tt=tt: conv_a2_dve(j, tt)))
            side.append((lambda j=j, tt=tt: conv_b_pe(j, tt, h0), lambda j=j, tt=tt: conv_b_dve1(j, tt)))
            side.append((lambda j=j, tt=tt: conv_b_pe(j, tt, h1), lambda j=j, tt=tt: conv_b_dve2(j, tt)))
            if tt % 2 == 1 and ada_left:
                ci = ada_left.pop(0)
                side.append((lambda j=j, tt=tt, ci=ci: (conv_b_stat(j, tt), ada_chunk(ci, with_dma=False, part=0)), nop))
                side.append((lambda ci=ci: ada_chunk(ci, with_dma=False, part=1), nop))
            else:
                if tt % 2 == 0 and ada_left:
                    side.append((lambda j=j, tt=tt, ci=ada_left[0]: (conv_b_stat(j, tt), ada_dma(ci)), nop))
                else:
                    side.append((lambda j=j, tt=tt: conv_b_stat(j, tt), nop))
    side.append((lambda: ada_finish(2), nop))

    QB = 6
    sqh_v, sqh_r = view(61 * KB, [128, 256], BF16)
    pqs_v, pqs_r = view(61 * KB + 512, [128, 256], F32)
    lnh_v, lnh_r = view(62 * KB + 512, [128, 256], F32)
    sqh_b, pqs_b, lnh_b = abuf("sqh", sqh_r), abuf("pqs", pqs_r), abuf("lnh", lnh_r)
    pqh = bank(QB)[:, 0:256]
    psh = bank(QB)[:, 256:512]

    def qkh_e1_pe(f_, c, ht):
        wv, wb = wsl[c][f_]
        for kc in range(8):
            mm(pqh, wv[:, kc, :], hT_v[:, kc, ht * 256:(ht + 1) * 256], kc == 0, kc == 7, [wb, hT_b[ht // 2]], [bankb[QB]])

    def qkh_e1_def(f_, c, ht):
        op("dve", lambda e: e.tensor_copy(pqs_v, pqh), [bankb[QB]], [pqs_b])
        op("dve", lambda e: e.tensor_tensor(out=sqh_v, in0=pqs_v, in1=pqs_v, op=ALU.mult), [pqs_b], [sqh_b])

    def qkh_e2_pe(f_, c, ht):
        mm(psh, BLK, sqh_v, True, True, [b_cbf, sqh_b], [bankb[QB]], skip_group_check=True)

    def qkh_e2_def(f_, c, ht):
        dst_v, dst_b, gcol = QK_FAM[f_]
        act(lnh_v, psh, AF.Ln, [bankb[QB], b_c64], [lnh_b], bias=small[:, SM_C64:SM_C64 + 1])
        act(lnh_v, lnh_v, AF.Exp, [lnh_b], [lnh_b], scale=-0.5)
        dst = dst_v[:, c, ht * 256:(ht + 1) * 256]
        op("dve", lambda e: e.scalar_tensor_tensor(out=dst, in0=pqs_v, scalar=vecs[:, gcol:gcol + 1], in1=lnh_v, op0=ALU.mult, op1=ALU.mult),
           [pqs_b, b_vecs, lnh_b], [dst_b[c]])

    side_qk = []
    for c in (1, 2, 3):
        for f_ in range(2):
            for ht in range(8):
                side_qk.append((lambda f_=f_, c=c, ht=ht: qkh_e1_pe(f_, c, ht), lambda f_=f_, c=c, ht=ht: qkh_e1_def(f_, c, ht)))
                side_qk.append((lambda f_=f_, c=c, ht=ht: qkh_e2_pe(f_, c, ht), lambda f_=f_, c=c, ht=ht: qkh_e2_def(f_, c, ht)))

    ya_v, ya_r = view(128 * KB, [128, 4, S], BF16)
    ya_bd = {}

    def ya_buf(c):
        if c not in ya_bd:
            ya_bd[c] = abuf("ya%d" % c, (128 * KB + c * 4 * KB, 128 * KB + (c + 1) * 4 * KB))
        return ya_bd[c]
    u_v, u_b, sp_v, sp_b, a_v, a_b = [], [], [], [], [], []
    for i in range(3):
        v_, r_ = view((160 + 4 * i) * KB, [128, 2, 512], F32)
        u_v.append(v_); u_b.append(abuf("u%d" % i, r_))
    for i in range(2):
        v_, r_ = view((172 + 2 * i) * KB, [128, 2, 512], BF16)
        sp_v.append(v_); sp_b.append(abuf("sp%d" % i, r_))
        v_, r_ = view((176 + 2 * i) * KB, [128, 2, 512], BF16)
        a_v.append(v_); a_b.append(abuf("a%d" % i, r_))
    r_vs, r_bs = [], []
    for i in range(2):
        v_, r_ = view((180 + 2 * i) * KB, [128, 2, 512], BF16)
        r_vs.append(v_); r_bs.append(abuf("r%d" % i, r_))
    yasq_v, yasq_r = view(60 * KB, [128, 512], BF16)
    yasq_b = abuf("yasq", yasq_r)
    ZP = pair(0)
    ZPb = [bankb[0], bankb[1]]
    BP = pair(1)
    BPb = [bankb[2], bankb[3]]
    OBS = [4, 4]
    NEGM = cbf[:, C_NEGM:C_NEGM + 128]

    units = []
    for c in range(4):
        for i in range(NTT):
            kbs = list(range(4 * i + 3, -1, -1))
            for n_, kb in enumerate(kbs):
                units.append(dict(c=c, i=i, kb=kb, first=(n_ == 0), last=(kb == 0), qlo=max(0, kb - 4 * i) * 128, diag=(kb >= 4 * i),
                                  ob=OBS[(c * NTT + i) % 2]))
    NU = len(units)

    def stA(n):
        U = units[n]
        c, i, kb, qlo = U["c"], U["i"], U["kb"], U["qlo"]
        for h in range(2):
            mm(ZP[:, h, qlo:512], k_v[64 * h:64 * h + 64, c, kb * 128:(kb + 1) * 128], q_v[64 * h:64 * h + 64, c, i * 512 + qlo:(i + 1) * 512],
               True, not U["diag"], [k_b[c], q_b[c]], [ZPb[h]], tile_position=(64 * h, 0), skip_group_check=True)
            if U["diag"]:
                mm(ZP[:, h, qlo:qlo + 128], ident, NEGM, False, True, [b_cbf], [ZPb[h]], skip_group_check=True)

    def stB1(n):
        U = units[n]
        qlo = U["qlo"]
        act(u_v[n % 3][:, :, qlo:512], ZP[:, :, qlo:512], AF.Exp, ZPb, [u_b[n % 3]], scale=8.0)

    def stB2(n):
        U = units[n]
        qlo = U["qlo"]
        act(sp_v[n % 2][:, :, qlo:512], u_v[n % 3][:, :, qlo:512], AF.Ln, [u_b[n % 3]], [sp_b[n % 2]], bias=1.0)

    def stC_pe(n):
        U = units[n]
        qlo = U["qlo"]
        for h in range(2):
            mm(BP[:, h, qlo:512], TRI, sp_v[n % 2][:, h, qlo:512], U["first"], False, [b_cbf, sp_b[n % 2]], [BPb[h]], skip_group_check=True)

    def stC_act(n):
        U = units[n]
        qlo = U["qlo"]
        act(r_vs[n % 2][:, :, qlo:512], BP[:, :, qlo:512], AF.Exp, BPb, [r_bs[n % 2]], scale=-1.0)

    def stD(n):
        U = units[n]
        qlo = U["qlo"]
        if not U["last"]:
            for h in range(2):
                mm(BP[:, h, qlo:512], COMP, sp_v[n % 2][:, h, qlo:512], False, False, [b_cbf, sp_b[n % 2]], [BPb[h]], skip_group_check=True)
        op("dve", lambda e, n=n, qlo=qlo: e.tensor_tensor(out=a_v[n % 2][:, :, qlo:512], in0=u_v[n % 3][:, :, qlo:512], in1=r_vs[n % 2][:, :, qlo:512], op=ALU.mult),
           [u_b[n % 3], r_bs[n % 2]], [a_b[n % 2]])

    def stE(n):
        U = units[n]
        c, i, kb, qlo = U["c"], U["i"], U["kb"], U["qlo"]
        OB = U["ob"]
        for h in range(2):
            hh = 2 * c + h
            mm(bank(OB)[64 * h:64 * h + 64, qlo:512], v_v[:, kb, hh * 64:(hh + 1) * 64], a_v[n % 2][:, h, qlo:512], U["first"], U["last"],
               [v_b, a_b[n % 2]], [bankb[OB]], tile_position=(0, 64 * h), skip_group_check=True)
        if U["last"]:
            dst = ya_v[:, c, i * 512:(i + 1) * 512]
            op("dve", lambda e, dst=dst: e.tensor_copy(dst, bank(OB)), [bankb[OB]], [ya_buf(c)])
            op("dve", lambda e, dst=dst: e.tensor_tensor(out=yasq_v, in0=dst, in1=dst, op=ALU.mult), [ya_buf(c)], [yasq_b])
            pe_later.append([2, lambda i=i: [stat_mm(16 + 4 * i + tbi, yasq_v[:, tbi * 128:(tbi + 1) * 128], [yasq_b]) for tbi in range(4)]])

    pe_later = []

    def junk_mm(k):
        for _ in range(k):
            mm(bank(PSTAT)[:, 64:512], ident, cbf[:, 0:448], False, False, [b_cbf], [bankb[PSTAT]], skip_group_check=True)

    deferred = [[]]
    deferred_late = [[]]
    for step in range(NU + 4):
        for (fn, lag) in ((stD, 3), (stC_pe, 2), (stE, 4)):
            n = step - lag
            if 0 <= n < NU:
                fn(n)
        for ent in list(pe_later):
            if ent[0] == 0:
                ent[1]()
                pe_later.remove(ent)
            else:
                ent[0] -= 1
        for f in deferred.pop(0):
            f()
        if 0 <= step - 1 < NU:
            stB1(step - 1)
        for f in deferred_late.pop(0):
            f()
        for (fn, lag) in ((stC_act, 2), (stB2, 1)):
            n = step - lag
            if 0 <= n < NU:
                fn(n)
        if 0 <= step < NU:
            stA(step)
        late, late2 = [], []
        if step >= 1:
            if side:
                pe_part, late_part = side.pop(0)
                pe_part()
                late.append(late_part)
            if side_qk and step % 4 != 3:
                pe_part, late_part = side_qk.pop(0)
                pe_part()
                late2.append(late_part)
        if not late and not late2:
            if not wo32_holder and not side and not side_qk:
                wo32_task()
            junk_mm(NJUNK)
        deferred.append(late)
        deferred_late.append(late2)
    for f in deferred_late.pop(0):
        f()
    for f in deferred.pop(0):
        f()
    assert not side_qk
    for ent in pe_later:
        ent[1]()
    while side:
        pe_part, late_part = side.pop(0)
        pe_part()
        late_part()
    if not wo32_holder:
        wo32_task()
    ya_b = [ya_buf(c) for c in range(4)]
    wo32_v, wo32_b = wo32_holder["v"], wo32_holder["b"]
    if dbg:
        dbg_out("yc", [128, 4, S], BF16)
        dbg_out("ya", [128, 4, S], BF16)
        for c in range(4):
            dma("sp", dbg_d["yc"][:, c, :], yc_v[:, c, :], reads=[yc_b])
            dma("sp", dbg_d["ya"][:, c, :], ya_v[:, c, :], reads=ya_b)
    if stage <= 3:
        return finish()

    ffw_off = [144 * KB, 112 * KB]

    def ffn_weight_views(slot):
        w1v, w1r = view(ffw_off[slot], [128, 8, 1024], BF16)
        w2v, w2r = view(ffw_off[slot] + 16 * KB, [128, 8, 1024], BF16)
        return w1v, w1r, w2v, w2r

    def load_ffn_weights(qtr, slot):
        w1v, w1r, w2v, w2r = ffn_weight_views(slot)
        b1, b2 = abuf("w1q%d" % qtr, w1r), abuf("w2q%d" % qtr, w2r)
        for hh in range(2):
            dma("pool", w1v[:, 4 * hh:4 * hh + 4, :],
                wff1_d[512 * hh:512 * hh + 512, qtr * 1024:(qtr + 1) * 1024].rearrange("(kc p) n -> p kc n", p=128), writes=[b1])
        for hh in range(2):
            dma("pool", w2v[:, 4 * hh:4 * hh + 4, :],
                wff2_d[qtr * 1024 + 512 * hh:qtr * 1024 + 512 * hh + 512, :].rearrange("(kc p) n -> p kc n", p=128), writes=[b2])
        return w1v, b1, w2v, b2

    ffw = {0: load_ffn_weights(0, 0)}

    act(small[:, SM_TMP2:SM_TMP2 + 32], bank(PSTAT)[:, 0:32], AF.Ln, [bankb[PSTAT], b_eps], [b_small["tmp2"]],
        scale=1.0 / 512, bias=small[:, SM_EPS:SM_EPS + 1])
    act(small[:, SM_RSM:SM_RSM + 32], small[:, SM_TMP2:SM_TMP2 + 32], AF.Exp, [b_small["tmp2"]], [b_small["rsm"]], scale=-0.5)
    wog_v, wog_r = view(96 * KB, [128, 8, D], BF16)
    wog_b = abufs(["wog%d" % kc for kc in range(8)], wog_r)
    for kc in range(8):
        op("dve", lambda e, kc=kc: e.scalar_tensor_tensor(out=wog_v[:, kc, :], in0=wo32_v[:, kc, :], scalar=vecs[:, V_GMIX + kc:V_GMIX + kc + 1],
                                                          in1=g1b_v, op0=ALU.mult, op1=ALU.mult), [wo32_b, b_vecs, b_g1b], [wog_b[kc]])
    xs2_v, xs2_b = [], []
    for i in range(2):
        v_, r_ = view((176 + 4 * i) * KB, [128, D], F32)
        xs2_v.append(v_)
        xs2_b.append(abuf("xs2_%d" % i, r_))
    x1_v, x1_r = view(0, [128, NTB, D], F32)
    x1_b = [abuf("x1_%d" % tb, (tb * 4 * KB, (tb + 1) * 4 * KB)) for tb in range(NTB)]
    junk2_v, junk2_r = view(184 * KB, [128, D], BF16)
    junk2_b = abuf("junk2", junk2_r)
    _n2s, _n2t = make_norm(2, None, None, None, [None], [None], None)
    for tb in range(NTB):
        i = tb % 2
        dma("sp", xs2_v[i], x_d[tb * 128:(tb + 1) * 128, :], writes=[xs2_b[i]])
        for fh in range(2):
            bc, ba = (0, 1) if fh == 0 else (2, 3)
            for j in range(4):
                mm(bank(bc), yc_v[:, j, tb * 128:(tb + 1) * 128], wog_v[:, j, fh * 512:(fh + 1) * 512], j == 0, j == 3, [yc_b, wog_b[j]], [bankb[bc]])
            for c in range(4):
                mm(bank(ba), ya_v[:, c, tb * 128:(tb + 1) * 128], wog_v[:, 4 + c, fh * 512:(fh + 1) * 512], c == 0, c == 3, ya_b + [wog_b[4 + c]], [bankb[ba]])
            dst = x1_v[:, tb, fh * 512:(fh + 1) * 512]
            op("dve", lambda e, dst=dst, bc=bc, tb=tb, i=i, fh=fh: e.scalar_tensor_tensor(
                out=dst, in0=bank(bc), scalar=small[:, SM_RSM + tb:SM_RSM + tb + 1], in1=xs2_v[i][:, fh * 512:(fh + 1) * 512],
                op0=ALU.mult, op1=ALU.add), [bankb[bc], b_small["rsm"], xs2_b[i]], [x1_b[tb]])
            op("dve", lambda e, dst=dst, ba=ba, tb=tb: e.scalar_tensor_tensor(
                out=dst, in0=bank(ba), scalar=small[:, SM_RSM + 16 + tb:SM_RSM + 16 + tb + 1], in1=dst,
                op0=ALU.mult, op1=ALU.add), [bankb[ba], b_small["rsm"], x1_b[tb]], [x1_b[tb]])
        _n2t.stats_act(tb, x1_v[:, tb, :], x1_b[tb], junk2_v, junk2_b)
    if dbg:
        dbg_out("x1", [128, NTB, D], F32)
        for tb in range(NTB):
            dma("sp", dbg_d["x1"][:, tb, :], x1_v[:, tb, :], reads=[x1_b[tb]])
    if stage <= 4:
        return finish()

    h2T_v, h2T_r = view(64 * KB, [128, 8, S], BF16)
    h2T_b = abufs(["h2T%d" % tg for tg in range(NTT)], h2T_r)
    xn2_v, xn2_r = view(176 * KB, [128, 4, D], BF16)
    xn2_b = abuf("xn2", xn2_r)
    ffw[1] = load_ffn_weights(1, 1)
    n2_stats, n2_transp = make_norm(2, h2T_v, h2T_b, lambda tb: (x1_v[:, tb, :], x1_b[tb]), [xn2_v], [xn2_b],
                                    lambda tg, tbi: (xn2_v[:, tbi, :], xn2_b))
    gT_v, gT_b = [], []
    for i in range(2):
        v_, r_ = view((96 + 8 * i) * KB, [128, 8, 512], BF16)
        gT_v.append(v_)
        gT_b.append(abuf("gT%d" % i, r_))
    r32_v, r32_b, tmp_v, tmp_b = [], [], [], []
    for i in range(2):
        v_, r_ = view((184 + 2 * i) * KB, [128, 512], F32)
        r32_v.append(v_); r32_b.append(abuf("r32_%d" % i, r_))
        v_, r_ = view((188 + 2 * i) * KB, [128, 512], F32)
        tmp_v.append(v_); tmp_b.append(abuf("tmpf_%d" % i, r_))
    cnt = dict(f1=0, f2=0)

    def ffn1(qtr, tt):
        w1v, b1, _, _ = ffw[qtr]
        g = (qtr * NTT + tt) % 2
        for ffc in range(8):
            bk = cnt["f1"] % 3
            ri = cnt["f1"] % 2
            cnt["f1"] += 1
            for kc in range(8):
                mm(bank(bk), w1v[:, kc, ffc * 128:(ffc + 1) * 128], h2T_v[:, kc, tt * 512:(tt + 1) * 512], kc == 0, kc == 7,
                   [b1, h2T_b[tt]], [bankb[bk]])
            act(r32_v[ri], bank(bk), AF.Relu, [bankb[bk]], [r32_b[ri]])
            op("dve", lambda e, g=g, ffc=ffc, ri=ri: e.tensor_tensor(out=gT_v[g][:, ffc, :], in0=r32_v[ri], in1=r32_v[ri], op=ALU.mult),
               [r32_b[ri]], [gT_b[g]])

    def ffn2(qtr, tt):
        _, _, w2v, b2 = ffw[qtr]
        g = (qtr * NTT + tt) % 2
        for tbi in range(4):
            tb = 4 * tt + tbi
            for fh in range(2):
                bk = 3 + cnt["f2"] % 3
                ti = cnt["f2"] % 2
                cnt["f2"] += 1
                for ffc in range(8):
                    mm(bank(bk), gT_v[g][:, ffc, tbi * 128:(tbi + 1) * 128], w2v[:, ffc, fh * 512:(fh + 1) * 512], ffc == 0, ffc == 7,
                       [gT_b[g], b2], [bankb[bk]])
                op("dve", lambda e, bk=bk, ti=ti, fh=fh: e.tensor_tensor(out=tmp_v[ti], in0=bank(bk), in1=gate2_b[:, fh * 512:(fh + 1) * 512], op=ALU.mult),
                   [bankb[bk], b_g2b], [tmp_b[ti]])
                dst = x1_v[:, tb, fh * 512:(fh + 1) * 512]
                op(ADD_ENG, lambda e, dst=dst, ti=ti: e.tensor_tensor(out=dst, in0=dst, in1=tmp_v[ti], op=ALU.add), [x1_b[tb], tmp_b[ti]], [x1_b[tb]])
            if qtr == 3:
                dma("sp", out_d[tb * 128:(tb + 1) * 128, :], x1_v[:, tb, :], reads=[x1_b[tb]])

    steps = [(qtr, tt) for qtr in range(4) for tt in range(NTT)]
    n2_stats(0, skip_act=True)
    n2_transp(0, banks=(6, 7))
    n2_stats(1, skip_act=True)
    for s_, (qtr, tt) in enumerate(steps):
        ffn1(qtr, tt)
        if qtr == 0 and tt + 1 < NTT:
            n2_transp(tt + 1, banks=(6, 7))
            if tt + 2 < NTT:
                n2_stats(tt + 2, skip_act=True)
        if s_ >= 1:
            pq, pt = steps[s_ - 1]
            ffn2(pq, pt)
            if pt == NTT - 1 and pq + 2 < 4:
                ffw[pq + 2] = load_ffn_weights(pq + 2, pq % 2)
    ffn2(*steps[-1])

    return finish()


def prep_inputs(inputs):
    f = lambda a: np.ascontiguousarray(np.asarray(a, dtype=np.float32))
    x = f(inputs["x"])
    c = f(inputs["c"])
    b_ada = f(inputs["b_ada"])
    conv_w = f(inputs["conv_w"])
    shared = {
        "b_ada": b_ada,
        "w_ada": f(inputs["w_ada"]),
        "w_in": f(inputs["w_in"]),
        "w_out": f(inputs["w_out"]),
        "w_ff1": f(inputs["w_ff1"]),
        "w_ff2": f(inputs["w_ff2"]),
    }
    cb = np.zeros((128, 768), np.float32)
    ar = np.arange(128)
    cb[:, C_ID:C_ID + 128] = np.eye(128)
    cb[:, C_TRI:C_TRI + 128] = (ar[:, None] >= ar[None, :])
    cb[:, C_COMP:C_COMP + 128] = (ar[:, None] < ar[None, :])
    cb[:, C_BLK:C_BLK + 128] = ((ar[:, None] // 64) == (ar[None, :] // 64))
    cb[:, C_ONES:C_ONES + 128] = 1.0
    cb[:, C_NEGM:C_NEGM + 128] = np.where(ar[:, None] >= ar[None, :], -30000.0, 0.0)
    shared["cbf"] = cb.astype(ml_dtypes.bfloat16)
    shared["mask"] = (ar[:, None] < ar[None, :]).astype(np.float32)
    col = lambda v, n: np.asarray(v, np.float32).reshape(n, 128).T
    in_maps = []
    for b in range(x.shape[0]):
        vecs = np.zeros((128, NV), np.float32)
        vecs[:, V_C:V_C + 8] = col(c[b], 8)
        vecs[:, V_BADA:V_BADA + 48] = col(b_ada, 48)
        vecs[:, V_G1:V_G1 + 8] = col(inputs["norm1_g"], 8)
        vecs[:, V_G2:V_G2 + 8] = col(inputs["norm2_g"], 8)
        vecs[:, V_CONV:V_CONV + 12] = conv_w.reshape(3, 4, 128).transpose(2, 1, 0).reshape(128, 12)
        vecs[:, V_GQ] = np.tile(f(inputs["q_norm_g"]), 2)
        vecs[:, V_GK] = np.tile(f(inputs["k_norm_g"]), 2)
        vecs[:, V_GMIX:V_GMIX + 8] = col(np.concatenate([f(inputs["conv_out_g"]), f(inputs["attn_out_g"])]), 8)
        m = dict(shared)
        m["x"] = x[b]
        m["vecs"] = vecs
        in_maps.append(m)
    return in_maps


_CACHE = {}


def kernel(**inputs):
    in_maps = prep_inputs(inputs)
    if "nc" not in _CACHE:
        nc = bass.Bass("TRN2", target_bir_lowering=False)
        build(nc)
        _CACHE["nc"] = nc
    nc = _CACHE["nc"]
    res = run_bass_kernel_spmd(nc, in_maps, core_ids=list(range(8)))
    out = np.stack([np.asarray(r["out"], dtype=np.float32) for r in res.results], axis=0)
    return out
```
